# Optimizing a Trainium2 kernel written in Bass

```python
import functools
import jax, jax.numpy as jnp
from jax import lax
import numpy as np


D_MODEL = 1024
BATCH = 16
SEQ = 2048
DEPTH = 2

GRID_W = 64
CTX_LEN = 256
N_MIXERS = 2
N_MOD = 6
EPS = 1e-6
GLA_HEADS = 4
GLA_KEY_DIM = D_MODEL // 2
GLA_VAL_DIM = D_MODEL
GLA_HEAD_K = GLA_KEY_DIM // GLA_HEADS
GLA_HEAD_V = GLA_VAL_DIM // GLA_HEADS
GLA_GATE_RANK = 16
GLA_GATE_TAU = 16.0
GLA_CHUNK = 64
GLA_IN_DIM = 2 * GLA_KEY_DIM + 2 * GLA_VAL_DIM + 2 * GLA_GATE_RANK
SC_DIM = D_MODEL
CONV_WIDTH = 3
FFN_HIDDEN = 5 * D_MODEL // 2

kernel_name = 'hybrid_gla_shortconv_convffn_dit'


def rmsnorm(x, gain):
    x32 = x.astype(jnp.float32)
    y = x32 * lax.rsqrt(jnp.mean(x32 * x32, axis=-1, keepdims=True) + EPS)
    return y.astype(x.dtype) * gain


def modulate(x, gain, shift, scale):
    return rmsnorm(x, gain) * (1 + scale) + shift


def dwconv3(u, w, axis):
    n = u.shape[axis]
    pad = [(0, 0)] * u.ndim
    pad[axis] = (1, 1)
    up = jnp.pad(u, pad)
    out = lax.slice_in_dim(up, 0, n, axis=axis) * w[0]
    for tap in range(1, CONV_WIDTH):
        out = out + lax.slice_in_dim(up, tap, tap + n, axis=axis) * w[tap]
    return out


def conv_grid(u, w, rows, axis):
    b, t, ch = u.shape
    return dwconv3(u.reshape(b, rows, GRID_W, ch), w, axis).reshape(b, t, ch)


def conv_seq(u, w):
    return dwconv3(u, w, 1)


def heads(t, dh):
    return t.reshape(t.shape[0], t.shape[1], -1, dh)


def gla_log_decay(a_low, w_a2, b_a):
    z = (a_low @ w_a2 + b_a).astype(jnp.float32)
    return heads(jax.nn.log_sigmoid(z) / GLA_GATE_TAU, GLA_HEAD_K)


def gla_scan(q, k, v, log_a, s0):
    bsz, t, nh, _ = q.shape
    dv = v.shape[-1]
    n = t // GLA_CHUNK

    def to_chunks(a):
        return a.astype(jnp.float32).reshape(bsz, n, GLA_CHUNK, nh, a.shape[-1]).transpose(1, 0, 3, 2, 4)

    xs = tuple(to_chunks(a) for a in (q, k, v, log_a))
    mask = jnp.tril(jnp.ones((GLA_CHUNK, GLA_CHUNK), dtype=bool))

    def step(s, inp):
        qi, ki, vi, gi = inp
        bcum = jnp.cumsum(gi, axis=-2)
        b_last = bcum[..., -1:, :]
        q_s = qi * jnp.exp(bcum)
        k_s = ki * jnp.exp(-bcum)
        k_d = ki * jnp.exp(b_last - bcum)
        att = jnp.where(mask, jnp.einsum('bhik,bhjk->bhij', q_s, k_s), 0.0)
        o = jnp.einsum('bhik,bhkv->bhiv', q_s, s) + jnp.einsum('bhij,bhjv->bhiv', att, vi)
        s_new = jnp.exp(b_last[..., 0, :])[..., None] * s + jnp.einsum('bhjk,bhjv->bhkv', k_d, vi)
        return s_new, o

    s_fin, oc = lax.scan(step, s0.astype(jnp.float32), xs)
    o = oc.transpose(1, 0, 3, 2, 4).reshape(bsz, t, nh, dv)
    return o, s_fin


def gla_state(k, v, log_a):
    bcum = jnp.cumsum(log_a, axis=1)
    k_d = k.astype(jnp.float32) * jnp.exp(bcum[:, -1:] - bcum)
    return jnp.einsum('bthk,bthv->bhkv', k_d, v.astype(jnp.float32))


def gla_split_cols(p):
    kt, vt, r = GLA_KEY_DIM, GLA_VAL_DIM, GLA_GATE_RANK
    return jnp.split(p, [kt, 2 * kt, 2 * kt + vt, 2 * kt + 2 * vt, 2 * kt + 2 * vt + r], axis=-1)


def gla_mixer(h, w_in, w_a2, b_a, head_gain, w_out, s0_f, s0_b):
    q, k, v, g, a_f, a_b = gla_split_cols(h @ w_in)
    q = heads(q, GLA_HEAD_K) * (GLA_HEAD_K ** -0.5)
    k = heads(k, GLA_HEAD_K)
    v = heads(v, GLA_HEAD_V)
    la_f = gla_log_decay(a_f, w_a2[0], b_a[0])
    la_b = gla_log_decay(a_b, w_a2[1], b_a[1])
    o_f, s_f = gla_scan(q, k, v, la_f, s0_f)
    flip = functools.partial(jnp.flip, axis=1)
    o_b, s_b = gla_scan(flip(q), flip(k), flip(v), flip(la_b), s0_b)
    o = o_f + flip(o_b)
    o = o * lax.rsqrt(jnp.mean(o * o, axis=-1, keepdims=True) + EPS)
    o = (o.astype(h.dtype) * head_gain).reshape(h.shape[0], h.shape[1], GLA_VAL_DIM)
    return (o * jax.nn.silu(g)) @ w_out, s_f, s_b


def gla_context_states(h, w_in, w_a2, b_a):
    kt, vt = GLA_KEY_DIM, GLA_VAL_DIM
    k, v = jnp.split(h @ w_in[:, kt:2 * kt + vt], [kt], axis=-1)
    a_f, a_b = jnp.split(h @ w_in[:, 2 * kt + 2 * vt:], 2, axis=-1)
    k = heads(k, GLA_HEAD_K)
    v = heads(v, GLA_HEAD_V)
    s_f = gla_state(k, v, gla_log_decay(a_f, w_a2[0], b_a[0]))
    s_b = gla_state(jnp.flip(k, 1), jnp.flip(v, 1), jnp.flip(gla_log_decay(a_b, w_a2[1], b_a[1]), 1))
    return s_f, s_b


def short_conv_mixer(h, w_in, conv_w, w_out, conv_fn):
    bg, cg, v = jnp.split(h @ w_in, 3, axis=-1)
    return (bg * conv_fn(cg * v, conv_w)) @ w_out


def conv_ffn(h, w_up, conv_w, conv_b, w_down, conv_fn):
    u = conv_fn(h @ w_up, conv_w) + conv_b
    a, gt = jnp.split(u, 2, axis=-1)
    return (a * jax.nn.silu(gt)) @ w_down


def setup_inputs(seed: int = 0) -> dict:
    key = jax.random.key(seed)
    ks = jax.random.split(key, 24)
    n_a = (DEPTH + N_MIXERS - 1) // N_MIXERS
    n_b = DEPTH // N_MIXERS
    f32 = jnp.float32

    def nrm(k, shape, scale):
        return jax.random.normal(k, shape, f32) * scale

    return {
        'x': nrm(ks[0], (BATCH, SEQ, D_MODEL), 1.0),
        'c': nrm(ks[1], (BATCH, D_MODEL), 1.0),
        'ctx': nrm(ks[2], (BATCH, CTX_LEN, D_MODEL), 1.0),
        'c_ctx': nrm(ks[3], (D_MODEL,), 1.0),
        'ada_w': nrm(ks[4], (DEPTH, D_MODEL, N_MOD * D_MODEL), 0.5 * D_MODEL ** -0.5),
        'ada_b': nrm(ks[5], (DEPTH, N_MOD * D_MODEL), 0.02),
        'norm_mix': 1.0 + nrm(ks[6], (DEPTH, D_MODEL), 0.02),
        'norm_ffn': 1.0 + nrm(ks[7], (DEPTH, D_MODEL), 0.02),
        'gla_w_in': nrm(ks[8], (n_a, D_MODEL, GLA_IN_DIM), D_MODEL ** -0.5),
        'gla_w_a2': nrm(ks[9], (n_a, 2, GLA_GATE_RANK, GLA_KEY_DIM), GLA_GATE_RANK ** -0.5),
        'gla_b_a': nrm(ks[10], (n_a, 2, GLA_KEY_DIM), 0.1),
        'gla_head_norm': 1.0 + nrm(ks[11], (n_a, GLA_HEAD_V), 0.02),
        'gla_w_out': nrm(ks[12], (n_a, GLA_VAL_DIM, D_MODEL), GLA_VAL_DIM ** -0.5),
        'sc_w_in': nrm(ks[13], (n_b, D_MODEL, 3 * SC_DIM), D_MODEL ** -0.5),
        'sc_conv_w': nrm(ks[14], (n_b, CONV_WIDTH, SC_DIM), CONV_WIDTH ** -0.5),
        'sc_w_out': nrm(ks[15], (n_b, SC_DIM, D_MODEL), SC_DIM ** -0.5),
        'ffn_w_up': nrm(ks[16], (DEPTH, D_MODEL, 2 * FFN_HIDDEN), D_MODEL ** -0.5),
        'ffn_conv_w': nrm(ks[17], (DEPTH, CONV_WIDTH, 2 * FFN_HIDDEN), CONV_WIDTH ** -0.5),
        'ffn_conv_b': nrm(ks[18], (DEPTH, 2 * FFN_HIDDEN), 0.02),
        'ffn_w_down': nrm(ks[19], (DEPTH, FFN_HIDDEN, D_MODEL), FFN_HIDDEN ** -0.5),
        'final_norm': 1.0 + nrm(ks[20], (D_MODEL,), 0.02),
    }


def reference(x, c, ctx, c_ctx, ada_w, ada_b, norm_mix, norm_ffn, gla_w_in, gla_w_a2, gla_b_a,
              gla_head_norm, gla_w_out, sc_w_in, sc_conv_w, sc_w_out, ffn_w_up, ffn_conv_w,
              ffn_conv_b, ffn_w_down, final_norm):
    rows = x.shape[1] // GRID_W
    conv_lat_rows = functools.partial(conv_grid, rows=rows, axis=2)
    conv_lat_cols = functools.partial(conv_grid, rows=rows, axis=1)
    h, hc = x, ctx
    sc, scc = jax.nn.silu(c), jax.nn.silu(c_ctx)
    for i in range(DEPTH):
        mixer, j = i % N_MIXERS, i // N_MIXERS
        ctx_later = any(l % N_MIXERS == 0 for l in range(i + 1, DEPTH))
        m = [t[:, None, :] for t in jnp.split(sc @ ada_w[i] + ada_b[i], N_MOD, axis=-1)]
        need_ctx = (mixer == 0) or ctx_later
        if need_ctx:
            mc = jnp.split(scc @ ada_w[i] + ada_b[i], N_MOD, axis=-1)
            hnc = modulate(hc, norm_mix[i], mc[0], mc[1])
        hn = modulate(h, norm_mix[i], m[0], m[1])
        if mixer == 0:
            if ctx_later:
                zero = jnp.zeros((hc.shape[0], GLA_HEADS, GLA_HEAD_K, GLA_HEAD_V), jnp.float32)
                yc, s_f, s_b = gla_mixer(hnc, gla_w_in[j], gla_w_a2[j], gla_b_a[j], gla_head_norm[j],
                                         gla_w_out[j], zero, zero)
            else:
                s_f, s_b = gla_context_states(hnc, gla_w_in[j], gla_w_a2[j], gla_b_a[j])
            y, _, _ = gla_mixer(hn, gla_w_in[j], gla_w_a2[j], gla_b_a[j], gla_head_norm[j],
                                gla_w_out[j], s_f, s_b)
        else:
            y = short_conv_mixer(hn, sc_w_in[j], sc_conv_w[j], sc_w_out[j], conv_lat_rows)
            if ctx_later:
                yc = short_conv_mixer(hnc, sc_w_in[j], sc_conv_w[j], sc_w_out[j], conv_seq)
        h = h + m[2] * y
        h = h + m[5] * conv_ffn(modulate(h, norm_ffn[i], m[3], m[4]), ffn_w_up[i], ffn_conv_w[i],
                                ffn_conv_b[i], ffn_w_down[i], conv_lat_cols)
        if ctx_later:
            hc = hc + mc[2] * yc
            hc = hc + mc[5] * conv_ffn(modulate(hc, norm_ffn[i], mc[3], mc[4]), ffn_w_up[i],
                                       ffn_conv_w[i], ffn_conv_b[i], ffn_w_down[i], conv_seq)
    return rmsnorm(h, final_norm)
```

```python
import types
import numpy as np
from contextlib import ExitStack
import concourse.bass as bass
import concourse.mybir as mybir
from concourse.bass_utils import run_bass_kernel_spmd

F32 = mybir.dt.float32
BF16 = mybir.dt.bfloat16
AF = mybir.ActivationFunctionType
ALU = mybir.AluOpType
AX = mybir.AxisListType

NCORES = 8
D = 1024
T = 2048
CT = 256
E = T + CT
NT = T // 128
NE = E // 128
KC = 8
FH = 2560
EPS = 1e-6
STOP = None
MARKS = []


def _freeze(fn):
    if fn is None or fn.__closure__ is None:
        return fn
    cells = []
    for c in fn.__closure__:
        try:
            cells.append(types.CellType(c.cell_contents))
        except ValueError:
            cells.append(c)
    return types.FunctionType(fn.__code__, fn.__globals__, fn.__name__, fn.__defaults__, tuple(cells))


class Buf:
    __slots__ = ("name", "lw", "rd")

    def __init__(self, name=""):
        self.name = name
        self.lw = None
        self.rd = []


class Sched:
    ENGS = ("pe", "act", "dve", "pool", "sp")

    def __init__(self, nc):
        self.nc = nc
        self.ops = {e: [] for e in self.ENGS}
        self.waited = {e: {} for e in self.ENGS}
        self.dma_cnt = {}

    def buf(self, name=""):
        return Buf(name)

    def bufs(self, n, name=""):
        return [Buf(f"{name}{i}") for i in range(n)]

    @staticmethod
    def _add_dep(deps, ref):
        if ref is None:
            return
        k, v = ref
        if deps.get(k, -1) < v:
            deps[k] = v

    def op(self, eng, fn, reads=(), writes=(), dma=None):
        fn = _freeze(fn)
        deps = {}
        for b in reads:
            self._add_dep(deps, b.lw)
        for b in writes:
            if b.lw is not None and (b.lw[0] != eng or dma is not None):
                self._add_dep(deps, b.lw)
            for r in b.rd:
                if r[0] != eng or dma is not None:
                    self._add_dep(deps, r)
        idx = len(self.ops[eng])
        waits = []
        wd = self.waited[eng]
        for k, v in deps.items():
            if k == eng and (eng == "pe" or dma is not None):
                continue
            if wd.get(k, -1) >= v:
                continue
            wd[k] = v
            waits.append((k, v))
        if dma is not None:
            c = self.dma_cnt.get(dma, 0)
            self.dma_cnt[dma] = c + 1
            ref = (("dma", dma), c)
        else:
            ref = (eng, idx)
        self.ops[eng].append({"fn": fn, "waits": waits, "dma": dma, "sig": False})
        for b in reads:
            b.rd.append(ref)
        for b in writes:
            b.lw = ref
            b.rd = []
        return ref

    def barrier(self):
        last = {}
        for e in self.ENGS:
            for i in range(len(self.ops[e]) - 1, -1, -1):
                if self.ops[e][i]["dma"] is None and self.ops[e][i]["fn"] is not None:
                    last[e] = i
                    break
        dl = {("dma", c): n - 1 for c, n in self.dma_cnt.items()}
        for e in self.ENGS:
            waits = []
            wd = self.waited[e]
            for k, v in list(last.items()) + list(dl.items()):
                if k == e:
                    continue
                if wd.get(k, -1) >= v:
                    continue
                wd[k] = v
                waits.append((k, v))
            if waits:
                self.ops[e].append({"fn": None, "waits": waits, "dma": None, "sig": False})

    def emit(self, stack):
        nc = self.nc
        for e in self.ENGS:
            for o in self.ops[e]:
                for (k, v) in o["waits"]:
                    if isinstance(k, str):
                        self.ops[k][v]["sig"] = True
        sigidx = {}
        for e in self.ENGS:
            c = 0
            arr = []
            for o in self.ops[e]:
                if o["sig"]:
                    c += 1
                arr.append(c)
            sigidx[e] = arr
        sems = {}
        for e in self.ENGS:
            sems[e] = stack.enter_context(nc.semaphore(f"s_{e}"))
        for c in self.dma_cnt:
            sems[("dma", c)] = stack.enter_context(nc.semaphore(f"d_{c}"))
        block = stack.enter_context(nc.Block())

        def run(e):
            def body(eng):
                for o in self.ops[e]:
                    for (k, v) in o["waits"]:
                        if isinstance(k, str):
                            eng.wait_ge(sems[k], sigidx[k][v])
                        else:
                            eng.wait_ge(sems[k], 16 * (v + 1))
                    if o["fn"] is None:
                        continue
                    ins = o["fn"](eng)
                    if o["dma"] is not None:
                        ins.then_inc(sems[("dma", o["dma"])], 16)
                    elif o["sig"]:
                        ins.then_inc(sems[e], 1)
            return body

        block.tensor(run("pe"))
        block.scalar(run("act"))
        block.vector(run("dve"))
        block.gpsimd(run("pool"))
        block.sync(run("sp"))
        return {e: len(self.ops[e]) for e in self.ENGS}


class Arena:
    def __init__(self, ar, nbytes):
        self.ar = ar
        self.n = nbytes
        self.limit = nbytes
        self.top = 0
        self.S = None

    def alloc(self, nbytes):
        nbytes = (nbytes + 63) // 64 * 64
        off = self.top
        self.top += nbytes
        assert self.top <= self.limit, f"arena overflow {self.top} > {self.limit}"
        return off

    def f32(self, n):
        off = self.alloc(n * 4)
        return self.ar[:, off // 4: off // 4 + n]

    def bf(self, n):
        off = self.alloc(n * 2)
        return self.ar[:, off // 4: off // 4 + n // 2].bitcast(BF16)

    def mark(self):
        return self.top

    def release(self, m):
        if self.S is not None:
            self.S.barrier()
        self.top = m


class ChainArena:
    def __init__(self, arenas):
        self.arenas = arenas

    def _pick(self, nbytes):
        nb = (nbytes + 63) // 64 * 64
        for a in self.arenas:
            if a.top + nb <= a.limit:
                return a
        raise AssertionError("chain arena overflow")

    def f32(self, n):
        return self._pick(n * 4).f32(n)

    def bf(self, n):
        return self._pick(n * 2).bf(n)


def build_nc(debug=False):
    nc = bass.Bass("TRN2", target_bir_lowering=False)

    def din(name, shape):
        return nc.dram_tensor(name, list(shape), F32, kind="ExternalInput").ap()

    x2 = din("x2", [2, T, D])
    ctx2 = din("ctx2", [2, CT, D])
    cT_d = din("cT", [128, KC * 3])
    adaw_d = din("ada_w", [2, D, 6 * D])
    adab_d = din("adab", [2, 128, 48])
    nmix_d = din("nmix", [2, 128, KC])
    nffn_d = din("nffn", [2, 128, KC])
    fnorm_d = din("fnorm", [128, D])
    gwin_d = din("gla_w_in", [D, 3104])
    wapad_d = din("wa_pad", [D, 64])
    wa2_d = din("wa2aug", [64, 512])
    hgT_d = din("hgT", [128, KC])
    gwout_d = din("gla_w_out", [D, D])
    scwin_d = din("sc_w_in", [D, 3 * D])
    scw_d = din("scw", [128, KC * 3])
    scwout_d = din("sc_w_out", [D, D])
    fup_d = din("ffn_w_up", [2, D, 2 * FH])
    fcw_d = din("fcw", [2, 128, 40 * 3])
    fcb_d = din("fcb", [2, 128, 40])
    fdn_d = din("ffn_w_down", [2, FH, D])
    ident_d = din("ident", [128, 128])
    ones_d = din("ones", [128, 128])
    mask_d = din("maskUL", [128, 256])
    out2 = nc.dram_tensor("out2", [2, T, D], F32, kind="ExternalOutput").ap()
    dbg = {}

    with ExitStack() as st:
        S = Sched(nc)
        ARN = 52736
        ar = st.enter_context(nc.sbuf_tensor("arena", [128, ARN], F32))
        pst = st.enter_context(nc.psum_tensor("pst", [128, 4096], F32))
        A = Arena(ar, ARN * 4)
        A.S = S
        HT_OFF = ARN * 4 - KC * T * 4

        def bank(b, n=512, off=0):
            return pst[:, b * 512 + off: b * 512 + off + n]

        pb = S.bufs(8, "psb")

        def dbg_dump(name, ap, shape, dt, rbufs):
            if not debug:
                return
            t = nc.dram_tensor("dbg_" + name, list(shape), dt, kind="ExternalOutput").ap()
            dbg[name] = t
            S.op("sp", lambda e: e.dma_start(out=t, in_=ap), reads=rbufs, dma="dbg_" + name)

        ident_f = A.f32(128); b_identf = S.buf()
        ident_b = A.bf(128); b_identb = S.buf()
        ones_b = A.bf(128); b_ones = S.buf()
        maskUL = A.f32(256); b_mask = S.buf()
        scanmask = A.bf(E); b_scanmask = S.buf()
        epsT = A.f32(1); b_eps = S.buf()
        onecol = A.f32(1); b_onecol = S.buf()
        cT = A.f32(KC * 3); b_cT = S.buf()
        scT = A.bf(KC * 4); b_scT = S.buf()
        adab = [A.f32(48) for _ in range(2)]; b_adab = S.buf()
        nmix = [A.f32(KC) for _ in range(2)]
        nffn = [A.f32(KC) for _ in range(2)]; b_gains = S.buf()
        fnorm = A.f32(D); b_fnorm = S.buf()
        hgT = A.f32(KC); b_hgT = S.buf()
        scw = A.f32(KC * 3); b_scw = S.buf()
        fcw = [A.f32(120) for _ in range(2)]
        fcb = [A.f32(40) for _ in range(2)]; b_fc = S.buf()
        wa2 = A.bf(512); b_wa2 = S.buf()
        mod = [A.f32(48 * 3) for _ in range(2)]; b_mod = S.buf()
        Amix = [A.f32(KC * 3) for _ in range(2)]
        Affn = [A.f32(KC * 3) for _ in range(2)]; b_Ader = S.buf()

        _ucls = [0]

        def _cls(c):
            if c in ("c0", "cc0"):
                _ucls[0] += 1
                return f"{c}_{_ucls[0]}"
            return c
        ld = lambda out, in_, wb, cls="c0": S.op("sp", lambda e: e.dma_start(out=out, in_=in_), writes=[wb], dma=_cls(cls))
        ldc = lambda out, in_, wb, cls: S.op("pool", lambda e: e.dma_start(out=out, in_=in_), writes=[wb], dma=_cls(cls))
        ld(ident_f, ident_d, b_identf)
        ld(maskUL, mask_d, b_mask)
        ld(cT, cT_d, b_cT)
        for l in range(2):
            ld(adab[l], adab_d[l], b_adab)
            ld(nmix[l], nmix_d[l], b_gains)
            ld(nffn[l], nffn_d[l], b_gains)
            ld(fcw[l], fcw_d[l], b_fc)
            ld(fcb[l], fcb_d[l], b_fc)
        ld(fnorm, fnorm_d, b_fnorm)
        ld(hgT, hgT_d, b_hgT)
        ld(scw, scw_d, b_scw)
        ldc(ident_b, ident_d, b_identb, "cc0")
        ldc(ones_b, ones_d, b_ones, "cc0")
        ldc(wa2[0:64, :], wa2_d, b_wa2, "cc0")
        S.op("dve", lambda e: e.memset(scanmask, 1.0), writes=[b_scanmask])
        smv = scanmask.rearrange("p (c t) -> p c t", t=128)
        S.op("dve", lambda e: e.memset(smv[:, :, 0:1], 0.0), writes=[b_scanmask])
        S.op("dve", lambda e: e.memset(epsT, EPS), writes=[b_eps])
        S.op("dve", lambda e: e.memset(onecol, 1.0), writes=[b_onecol])

        m0 = A.mark()
        S.op("act", lambda e: e.activation(out=scT.rearrange("p (k j) -> p k j", j=4)[:, :, 0:3],
                                           in_=cT.rearrange("p (k j) -> p k j", j=3), func=AF.Silu),
             reads=[b_cT], writes=[b_scT])
        scT3 = scT.rearrange("p (k j) -> p k j", j=4)
        adaW = [A.bf(KC * D) for _ in range(2)]
        b_adaW = S.bufs(2, "adaW")

        def ada_finish(l, pm, pbtok):
            modv = mod[l].rearrange("p (c j) -> p c j", j=3)
            S.op("dve", lambda e, pm=pm, modv=modv, l=l: e.tensor_tensor(
                out=modv, in0=pm[:, :, 0:3], in1=adab[l].unsqueeze(2).to_broadcast([128, 48, 3]), op=ALU.add),
                reads=[pbtok, b_adab], writes=[b_mod])
            for (dst, lo, gain) in ((Amix[l], 8, nmix[l]), (Affn[l], 32, nffn[l])):
                dv = dst.rearrange("p (c j) -> p c j", j=3)
                S.op("dve", lambda e, dv=dv, modv=modv, lo=lo, gain=gain: e.scalar_tensor_tensor(
                    out=dv, in0=modv[:, lo:lo + 8, :], scalar=1.0, in1=gain.unsqueeze(2).to_broadcast([128, 8, 3]),
                    op0=ALU.add, op1=ALU.mult), reads=[b_mod, b_gains], writes=[b_Ader])

        for l in range(1):
            pm = bank(l, 192).rearrange("p (c j) -> p c j", j=4)
            for k in range(6):
                s = (l * 6 + k) % 2
                wv = adaW[s].rearrange("p (kc n) -> p kc n", n=D)
                src = adaw_d[l].rearrange("(kc p) n -> p kc n", p=128)[:, :, k * D:(k + 1) * D]
                ldc(wv, src, b_adaW[s], f"adaW{s}")
                for mc in range(KC):
                    for kc in range(KC):
                        S.op("pe", lambda e, wv=wv, mc=mc, kc=kc, pm=pm, k=k: e.matmul(
                            pm[:, k * 8 + mc, 0:3], lhsT=wv[:, kc, mc * 128:(mc + 1) * 128], rhs=scT3[:, kc, 0:3],
                            start=(kc == 0), stop=(kc == KC - 1)),
                            reads=[b_adaW[s], b_scT], writes=[pb[l]])
            ada_finish(l, pm, pb[l])

        def make_ada1(adaH, b_adaH):
            pm1 = bank(3, 192).rearrange("p (c j) -> p c j", j=4)
            src1 = adaw_d[1].rearrange("(kc p) n -> p kc n", p=128)

            def dma(hp):
                if hp >= 12:
                    return
                k, half = hp // 2, hp % 2
                s_ = hp % 2
                wv = adaH[s_].rearrange("p (kc n) -> p kc n", n=512)
                ldc(wv, src1[:, :, k * D + half * 512:k * D + half * 512 + 512], b_adaH[s_], f"adaH{s_}")

            def mm(hp):
                k, half = hp // 2, hp % 2
                s_ = hp % 2
                wv = adaH[s_].rearrange("p (kc n) -> p kc n", n=512)
                for m4 in range(4):
                    mc = half * 4 + m4
                    for kc in range(KC):
                        S.op("pe", lambda e, wv=wv, m4=m4, mc=mc, kc=kc, k=k: e.matmul(
                            pm1[:, k * 8 + mc, 0:3], lhsT=wv[:, kc, m4 * 128:(m4 + 1) * 128], rhs=scT3[:, kc, 0:3],
                            start=(kc == 0), stop=(kc == KC - 1)),
                            reads=[b_adaH[s_], b_scT], writes=[pb[3]])

            def fin_part(c0, c1):
                modv = mod[1].rearrange("p (c j) -> p c j", j=3)
                S.op("dve", lambda e, modv=modv, c0=c0, c1=c1: e.tensor_tensor(
                    out=modv[:, c0:c1, :], in0=pm1[:, c0:c1, 0:3], in1=adab[1][:, c0:c1].unsqueeze(2).to_broadcast([128, c1 - c0, 3]), op=ALU.add),
                    reads=[pb[3], b_adab], writes=[b_mod])

            def fin_derived():
                modv = mod[1].rearrange("p (c j) -> p c j", j=3)
                for (dst, lo, gain) in ((Amix[1], 8, nmix[1]), (Affn[1], 32, nffn[1])):
                    dv = dst.rearrange("p (c j) -> p c j", j=3)
                    S.op("dve", lambda e, dv=dv, modv=modv, lo=lo, gain=gain: e.scalar_tensor_tensor(
                        out=dv, in0=modv[:, lo:lo + 8, :], scalar=1.0, in1=gain.unsqueeze(2).to_broadcast([128, 8, 3]),
                        op0=ALU.add, op1=ALU.mult), reads=[b_mod, b_gains], writes=[b_Ader])
            return dma, mm, fin_part, fin_derived

        A.release(m0)
        S.barrier()
        dbg_dump("mod0", mod[0], [128, 144], F32, [b_mod])
        dbg_dump("Amix0", Amix[0], [128, 24], F32, [b_Ader])

        def modcol(l, k, kc, j):
            return mod[l].rearrange("p (c j) -> p c j", j=3)[:, k * 8 + kc, j:j + 1]

        def acol(tbl, l, kc, j):
            return tbl[l].rearrange("p (c j) -> p c j", j=3)[:, kc, j:j + 1]

        hT = ar[:, HT_OFF // 4: HT_OFF // 4 + KC * T]
        A.limit = HT_OFF
        hT3 = hT.rearrange("p (kc t) -> p kc t", t=T)
        b_hT = [[S.buf() for _ in range(4)] for _ in range(KC)]
        base_mark = A.mark()

        def rstd_from_ss(ss_in, out, rb, wb, n_inv):
            S.op("act", lambda e: e.activation(out=out, in_=ss_in, func=AF.Ln, bias=epsT, scale=n_inv),
                 reads=rb + [b_eps], writes=[wb])
            S.op("act", lambda e: e.activation(out=out, in_=out, func=AF.Exp, scale=-0.5), reads=[wb], writes=[wb])

        def make_norm(j, Atbl, shift_k, l, hn3, b_hn, extra_w=None, pbanks=(0, 1), AL=None, pre_w=None):
            AL = AL if AL is not None else A
            sq = AL.bf(KC * 512); b_sq = S.bufs(2)
            sq3 = sq.rearrange("p (kc t) -> p kc t", t=512)
            rs = [AL.f32(512) for _ in range(2)]; b_rs = S.bufs(2)
            tmp = AL.f32(KC * 512); b_tmp = S.buf()
            tmp3 = tmp.rearrange("p (kc t) -> p kc t", t=512)
            n_tokens = b_sq + b_rs + [b_tmp]
            pre_left = {"act": True, "pool": True, "dve": True}

            def prew(eng):
                if pre_w is None or not pre_left[eng]:
                    return []
                pre_left[eng] = False
                return list(pre_w)

            def n1(blk):
                s = blk % 2
                sl = slice(blk * 512, (blk + 1) * 512)
                S.op("act", lambda e, sl=sl: e.activation(out=sq3[:, 0:5, :], in_=hT3[:, 0:5, sl], func=AF.Square),
                     reads=[b_hT[kc][blk] for kc in range(0, 5)], writes=[b_sq[0]] + prew("act"))
                S.op("pool", lambda e, sl=sl: e.tensor_tensor(out=sq3[:, 5:8, :], in0=hT3[:, 5:8, sl], in1=hT3[:, 5:8, sl], op=ALU.mult),
                     reads=[b_hT[kc][blk] for kc in range(5, 8)], writes=[b_sq[1]] + prew("pool"))
                bk = pbanks[s]
                for kc in range(KC):
                    S.op("pe", lambda e, kc=kc, bk=bk: e.matmul(bank(bk), lhsT=ones_b, rhs=sq3[:, kc, :],
                                                               start=(kc == 0), stop=(kc == KC - 1)),
                         reads=[b_sq[0 if kc < 5 else 1], b_ones], writes=[pb[bk]])
                rstd_from_ss(bank(bk), rs[s], [pb[bk]], b_rs[s], 1.0 / D)

            def n2(blk):
                s = blk % 2
                sl = slice(blk * 512, (blk + 1) * 512)
                S.op("dve", lambda e, sl=sl, s=s: e.tensor_tensor(
                    out=tmp3, in0=hT3[:, :, sl], in1=rs[s].unsqueeze(1).to_broadcast([128, KC, 512]), op=ALU.mult),
                    reads=[b_hT[kc][blk] for kc in range(KC)] + [b_rs[s]], writes=[b_tmp] + prew("dve"))
                first = {"act": True, "dve": True}
                for kc in range(KC):
                    eng = "dve" if kc in (3, 7) else "act"
                    wl = [b_hn[blk][0 if eng == "act" else 1]] + (extra_w[blk] if (extra_w is not None and first[eng]) else [])
                    first[eng] = False
                    if eng == "act":
                        S.op("act", lambda e, kc=kc, sl=sl: e.activation(
                            out=hn3[:, kc, sl], in_=tmp3[:, kc, :], func=AF.Identity,
                            scale=acol(Atbl, l, kc, j), bias=modcol(l, shift_k, kc, j)),
                            reads=[b_tmp, b_Ader, b_mod], writes=wl)
                    else:
                        S.op("dve", lambda e, kc=kc, sl=sl: e.tensor_scalar(
                            out=hn3[:, kc, sl], in0=tmp3[:, kc, :], scalar1=acol(Atbl, l, kc, j), scalar2=modcol(l, shift_k, kc, j),
                            op0=ALU.mult, op1=ALU.add), reads=[b_tmp, b_Ader, b_mod], writes=wl)
            return n1, n2, n_tokens

        def sub_arena(ap_bytes_off, nbytes):
            a2 = Arena(ar, A.n)
            a2.top = ap_bytes_off
            a2.limit = ap_bytes_off + nbytes
            return a2

        def run_norm(norm_args, AL, after_first=None, blocks=(0, 1, 2, 3)):
            n1, n2, toks = make_norm(*norm_args, AL=AL)
            bl = list(blocks)
            n1(bl[0])
            if after_first is not None:
                after_first()
            for i_ in range(1, len(bl)):
                n1(bl[i_]); n2(bl[i_ - 1])
            n2(bl[-1])
            return toks

        def conv_ffn(j, l, hn3, b_hn, norm_args=None, pre=None, tail_norm=None):
            m = A.mark()
            wup_off = A.top
            wup = [A.bf(KC * 256) for _ in range(3)]; b_wupP = [S.bufs(3), S.bufs(3)]
            if pre is not None:
                assert pre["off"] == wup_off, (pre["off"], wup_off)
                b_wupP = pre["tok"]
            wdn = [A.bf(20 * 128) for _ in range(2)]; b_wdn = S.bufs(2)
            act_off = A.top
            actT = A.bf(20 * 1024); actT3 = actT.rearrange("p (c t) -> p c t", t=1024)
            b_act = S.bufs(20)
            NACC = 4
            acc_off = A.top
            acc = [A.f32(1024) for _ in range(NACC)]; b_acc = S.bufs(NACC)
            sg = [A.f32(1024)]; b_sg = S.bufs(1)
            usb = [A.f32(1088) for _ in range(2)]; b_usb = S.bufs(2)
            acc_bytes = A.top - acc_off
            fcw3 = fcw[l].rearrange("p (c k) -> p c k", k=3)
            up_src = fup_d[l].rearrange("(kc p) n -> p kc n", p=128)
            dn_src = fdn_d[l].rearrange("(kc p) n -> p kc n", p=128)

            def load_up(g):
                if g >= 40:
                    return
                pj = g % 20
                ws = g % 3
                wv = wup[ws].rearrange("p (kc n) -> p kc n", n=256)
                ldc(wv[:, :, 0:128], up_src[:, :, pj * 128:(pj + 1) * 128], b_wupP[0][ws], f"wupA{ws}")
                ldc(wv[:, :, 128:256], up_src[:, :, FH + pj * 128:FH + (pj + 1) * 128], b_wupP[1][ws], f"wupG{ws}")

            def load_dn(gd):
                if gd >= 16:
                    return
                mc = gd % 8
                ds = gd % 2
                dv = wdn[ds].rearrange("p (kc n) -> p kc n", n=128)
                ldc(dv, dn_src[:, :, mc * 128:(mc + 1) * 128], b_wdn[ds], f"wdn{ds}")

            ntok = []
            if norm_args is not None:
                ntok = run_norm(norm_args, sub_arena(act_off, 20 * 1024 * 2), after_first=lambda: (load_up(0), load_up(1)))
            elif pre is None:
                load_up(0); load_up(1)
            ai = 0
            pending = []
            for hf in range(2):
                u0 = 0 if hf == 0 else 960
                for pj in range(20):
                    g = hf * 20 + pj
                    ws = g % 3
                    wv = wup[ws].rearrange("p (kc n) -> p kc n", n=256)
                    load_up(g + 2)
                    if pj == 16:
                        load_dn(hf * 8); load_dn(hf * 8 + 1)
                    accs = []
                    for part in range(2):
                        if part == 1 and pending:
                            pending.pop(0)()
                        pg = (pj * 2 + part) % 2
                        pbase = pg * 3
                        for kc in range(KC):
                            for nb, (c0, cn) in enumerate(((0, 512), (512, 512), (1024, 64))):
                                blkidx = sorted(set([(u0 + c0) // 512, (u0 + c0 + cn - 1) // 512]))
                                S.op("pe", lambda e, wv=wv, kc=kc, part=part, pbase=pbase, nb=nb, c0=c0, cn=cn, u0=u0: e.matmul(
                                    bank(pbase + nb, cn), lhsT=wv[:, kc, part * 128:(part + 1) * 128],
                                    rhs=hn3[:, kc, u0 + c0:u0 + c0 + cn], start=(kc == 0), stop=(kc == KC - 1)),
                                    reads=[b_wupP[part][ws]] + [t_ for b in blkidx for t_ in b_hn[b]], writes=[pb[pbase + nb]])
                        rb = [pb[pbase], pb[pbase + 1], pb[pbase + 2]]
                        ch = part * 20 + pj
                        a = ai % NACC
                        us = ai % 2
                        ai += 1
                        accs.append(a)
                        ups = usb[us]
                        ub = [b_usb[us]]
                        S.op("act", lambda e, ups=ups, pbase=pbase: e.copy(out=ups, in_=pst[:, pbase * 512: pbase * 512 + 1088]),
                             reads=rb, writes=ub)
                        ctr = 0 if hf == 0 else 64
                        S.op("act", lambda e, a=a, ups=ups, ctr=ctr, ch=ch: e.activation(
                            out=acc[a], in_=ups[:, ctr:ctr + 1024], func=AF.Identity,
                            scale=fcw3[:, ch, 1:2], bias=fcb[l][:, ch:ch + 1]),
                            reads=ub + [b_fc], writes=[b_acc[a]])
                        if hf == 0:
                            S.op("dve", lambda e, a=a, ups=ups, ch=ch: e.scalar_tensor_tensor(
                                out=acc[a][:, 64:1024], in0=ups[:, 0:960], scalar=fcw3[:, ch, 0:1], in1=acc[a][:, 64:1024],
                                op0=ALU.mult, op1=ALU.add), reads=ub + [b_fc, b_acc[a]], writes=[b_acc[a]])
                            S.op("dve", lambda e, a=a, ups=ups, ch=ch: e.scalar_tensor_tensor(
                                out=acc[a], in0=ups[:, 64:1088], scalar=fcw3[:, ch, 2:3], in1=acc[a],
                                op0=ALU.mult, op1=ALU.add), reads=ub + [b_fc, b_acc[a]], writes=[b_acc[a]])
                        else:
                            S.op("dve", lambda e, a=a, ups=ups, ch=ch: e.scalar_tensor_tensor(
                                out=acc[a], in0=ups[:, 0:1024], scalar=fcw3[:, ch, 0:1], in1=acc[a],
                                op0=ALU.mult, op1=ALU.add), reads=ub + [b_fc, b_acc[a]], writes=[b_acc[a]])
                            S.op("dve", lambda e, a=a, ups=ups, ch=ch: e.scalar_tensor_tensor(
                                out=acc[a][:, 0:960], in0=ups[:, 128:1088], scalar=fcw3[:, ch, 2:3], in1=acc[a][:, 0:960],
                                op0=ALU.mult, op1=ALU.add), reads=ub + [b_fc, b_acc[a]], writes=[b_acc[a]])
                    aa, ag = accs

                    def gate(aa=aa, ag=ag, pj=pj):
                        S.op("act", lambda e, ag=ag: e.activation(out=sg[0], in_=acc[ag], func=AF.Silu),
                             reads=[b_acc[ag]], writes=[b_sg[0]])
                        S.op("pool", lambda e, aa=aa, pj=pj: e.tensor_tensor(out=actT3[:, pj, :], in0=acc[aa], in1=sg[0], op=ALU.mult),
                             reads=[b_acc[aa], b_sg[0]], writes=[b_act[pj]] + ntok)
                    pending.append(gate)
                while pending:
                    pending.pop(0)()
                def dn_mm(mc, nb, bk, k0, k1):
                    ds = (hf * 8 + mc) % 2
                    dv = wdn[ds].rearrange("p (kc n) -> p kc n", n=128)
                    for kc in range(k0, k1):
                        S.op("pe", lambda e, dv=dv, kc=kc, nb=nb, bk=bk: e.matmul(
                            bank(bk), lhsT=dv[:, kc, :], rhs=actT3[:, kc, nb * 512:(nb + 1) * 512],
                            start=(kc == 0), stop=(kc == 19)),
                            reads=[b_wdn[ds], b_act[kc]], writes=[pb[bk]])

                def dn_evac(mc, nb, bk):
                    blk = hf * 2 + nb
                    sl = slice(blk * 512, (blk + 1) * 512)
                    S.op("dve", lambda e, bk=bk, mc=mc, sl=sl: e.scalar_tensor_tensor(
                        out=hT3[:, mc, sl], in0=bank(bk), scalar=modcol(l, 5, mc, j), in1=hT3[:, mc, sl],
                        op0=ALU.mult, op1=ALU.add), reads=[pb[bk], b_mod, b_hT[mc][blk]], writes=[b_hT[mc][blk]])

                head_groups = [(0, 0, 6), (0, 1, 7), (1, 0, 0)]
                for (mc, nb, bk) in head_groups:
                    dn_mm(mc, nb, bk, 0, 18)
                for (mc, nb, bk) in head_groups[0:2]:
                    dn_mm(mc, nb, bk, 18, 20)
                    dn_evac(mc, nb, bk)
                load_dn(hf * 8 + 2)
                dn_mm(1, 0, 0, 18, 20)
                dn_evac(1, 0, 0)
                dn_mm(1, 1, 1, 0, 20)
                dn_evac(1, 1, 1)
                load_dn(hf * 8 + 3)
                tn1 = tn2 = None
                if hf == 1 and tail_norm is not None:
                    tn1, tn2, _ = make_norm(*tail_norm, pbanks=(2, 3), AL=sub_arena(acc_off, acc_bytes),
                                            pre_w=b_acc + b_sg + b_usb)
                    tn1(0)
                for mc in range(2, KC):
                    gd = hf * 8 + mc
                    for nb in range(2):
                        bk = 6 + nb
                        dn_mm(mc, nb, bk, 0, 20)
                        dn_evac(mc, nb, bk)
                    if mc + 2 < KC:
                        load_dn(gd + 2)
                    if tn1 is not None:
                        if mc == 2:
                            tn1(1)
                        elif mc == 3:
                            tn2(0)
                        elif mc == 5:
                            tn2(1)
            A.release(m)

        class _Stop(Exception):
            pass

        def stop_at(tag):
            MARKS.append((tag, sum(1 for o in S.ops["pe"] if o["fn"] is not None)))
            if STOP == tag:
                raise _Stop()

        try:
          for j in range(2):
            S.barrier()
            A.release(base_mark)
            A.limit = A.n
            ogT = A.bf(KC * T); ogT3 = ogT.rearrange("p (kc t) -> p kc t", t=T)
            b_ogT = S.bufs(NT, "ogT")
            m_og = A.mark()
            hnE = A.bf(KC * E); hnE3 = hnE.rearrange("p (kc t) -> p kc t", t=E)
            b_hnE = S.bufs(NE, "hnE")
            mh = A.mark()
            xt = [A.f32(D) for _ in range(3)]; b_xt = S.bufs(3)
            junk = A.bf(D); b_junk = S.buf()
            xn = [A.bf(D) for _ in range(2)]; b_xn = S.bufs(2)
            ss = [A.f32(1) for _ in range(3)]; b_ss = S.bufs(3)
            rsd = [A.f32(1) for _ in range(3)]; b_rsd = S.bufs(3)

            def a1_load(e_):
                s = e_ % 3
                src = ctx2[j, e_ * 128:(e_ + 1) * 128, :] if e_ < 2 else x2[j, (e_ - 2) * 128:(e_ - 1) * 128, :]
                ld(xt[s], src, b_xt[s], f"xt{s}")

            def a1_stats(e_):
                s = e_ % 3
                s2 = e_ % 2
                S.op("act", lambda e, s=s: e.activation(out=junk, in_=xt[s], func=AF.Square, accum_out=ss[s]),
                     reads=[b_xt[s]], writes=[b_junk, b_ss[s]])
                rstd_from_ss(ss[s], rsd[s], [b_ss[s]], b_rsd[s], 1.0 / D)
                S.op("dve", lambda e, s=s, s2=s2: e.tensor_scalar(out=xn[s2], in0=xt[s], scalar1=rsd[s], scalar2=None, op0=ALU.mult),
                     reads=[b_xt[s], b_rsd[s]], writes=[b_xn[s2]])

            def a1_tr(e_):
                s2 = e_ % 2
                jj = 2 if e_ < 2 else j
                bk = 5 + (e_ % 3)
                ptb = bank(bk).bitcast(BF16).rearrange("p (kc t) -> p kc t", t=128)
                for kc in range(KC):
                    S.op("pe", lambda e, ptb=ptb, kc=kc, s2=s2: e.transpose(out=ptb[:, kc, :], in_=xn[s2][:, kc * 128:(kc + 1) * 128], identity=ident_b),
                         reads=[b_xn[s2], b_identb], writes=[pb[bk]])
                for kc in range(KC):
                    if kc < 2:
                        S.op("act", lambda e, ptb=ptb, kc=kc, e_=e_, jj=jj: e.activation(
                            out=hnE3[:, kc, e_ * 128:(e_ + 1) * 128], in_=ptb[:, kc, :], func=AF.Identity,
                            scale=acol(Amix, 0, kc, jj), bias=modcol(0, 0, kc, jj)),
                            reads=[pb[bk], b_Ader, b_mod], writes=[b_hnE[e_]])
                    else:
                        S.op("dve", lambda e, ptb=ptb, kc=kc, e_=e_, jj=jj: e.tensor_scalar(
                            out=hnE3[:, kc, e_ * 128:(e_ + 1) * 128], in0=ptb[:, kc, :],
                            scalar1=acol(Amix, 0, kc, jj), scalar2=modcol(0, 0, kc, jj), op0=ALU.mult, op1=ALU.add),
                            reads=[pb[bk], b_Ader, b_mod], writes=[b_hnE[e_]])

            a1_load(0); a1_load(1)
            if j == 0:
                adaH1 = [A.bf(KC * 512) for _ in range(2)]
                a_dma, a_mm, a_fin_part, _ = make_ada1(adaH1, S.bufs(2, "adaHa"))
                a_dma(0); a_dma(1)
            for s_ in range(NE + 1):
                if s_ < NE:
                    a1_stats(s_)
                if s_ >= 1:
                    a1_tr(s_ - 1)
                if s_ + 2 < NE:
                    a1_load(s_ + 2)
                if j == 0 and s_ in (6, 11, 16):
                    hp = {6: 0, 11: 2, 16: 4}[s_]
                    a_mm(hp); a_mm(hp + 1)
                    if hp + 2 < 6:
                        a_dma(hp + 2); a_dma(hp + 3)
            if j == 0:
                a_fin_part(0, 24)
            A.release(mh)
            dbg_dump(f"hnE{j}", hnE, [128, KC * E], BF16, b_hnE)
            stop_at("A1")

            stop_at("A1_")
            alow = A.bf(E); b_alow = S.buf()
            wap = A.bf(KC * 64); b_wap = S.buf()
            wap3 = wap.rearrange("p (kc n) -> p kc n", n=64)
            ldc(wap3, wapad_d.rearrange("(kc p) n -> p kc n", p=128), b_wap, "wap")
            S.op("dve", lambda e: e.memset(alow[0:64, :], 1.0), writes=[b_alow])
            eblocks = [(0, 512), (512, 512), (1024, 512), (1536, 512), (2048, 256)]
            for bi, (c0, cn) in enumerate(eblocks):
                for kc in range(KC):
                    S.op("pe", lambda e, bi=bi, c0=c0, cn=cn, kc=kc: e.matmul(
                        pst[0:64, bi * 512: bi * 512 + cn], lhsT=wap3[:, kc, :], rhs=hnE3[:, kc, c0:c0 + cn],
                        start=(kc == 0), stop=(kc == KC - 1)),
                        reads=[b_wap] + b_hnE[c0 // 128:(c0 + cn) // 128], writes=[pb[bi]])
                for r0 in (0, 32):
                    S.op("dve", lambda e, bi=bi, c0=c0, cn=cn, r0=r0: e.tensor_copy(
                        out=alow[r0:r0 + 16, c0:c0 + cn], in_=pst[r0:r0 + 16, bi * 512: bi * 512 + cn]),
                        reads=[pb[bi]], writes=[b_alow])

            wq = A.bf(KC * 128); wk = A.bf(KC * 128); wvv = A.bf(KC * 256); wg = A.bf(KC * 256)
            b_wq, b_wk, b_wv, b_wg = S.bufs(4)
            wsrc = gwin_d.rearrange("(kc p) n -> p kc n", p=128)
            dead_off = A.top
            qT = A.bf(T); b_qT = S.buf()
            kT = A.bf(E); b_kT = S.buf()
            tmpA = A.f32(E); b_tA = S.buf()
            tmpB = A.f32(E); b_tB = S.buf()
            assert A.top - dead_off >= KC * D * 2 and A.top <= HT_OFF
            wo = ar[:, dead_off // 4: dead_off // 4 + KC * D // 2].bitcast(BF16)
            wo3 = wo.rearrange("p (kc n) -> p kc n", n=D); b_wo = S.buf()
            wo_end = dead_off + KC * D * 2
            vh = A.bf(NE * 256); vh3 = vh.rearrange("p (e n) -> p e n", n=256); b_vh = S.bufs(NE)
            sgh = A.bf(NT * 256); sgh3 = sgh.rearrange("p (e n) -> p e n", n=256); b_sgh = S.bufs(NT)
            tots = A.f32(NE); b_tots = S.buf()
            dec = [A.f32(NE) for _ in range(2)]; b_dec = S.bufs(2)
            qs = [A.bf(T) for _ in range(2)]; b_qs = S.bufs(2)
            kdT = [A.bf(E) for _ in range(2)]; b_kdT = S.bufs(2)
            kd = [A.bf(NE * 128) for _ in range(2)]
            kd3 = [k_.rearrange("p (e n) -> p e n", n=128) for k_ in kd]
            b_kd = [S.bufs(NE) for _ in range(2)]
            Stil = [A.bf(NT * 256) for _ in range(2)]
            Stil3 = [s_.rearrange("p (e n) -> p e n", n=256) for s_ in Stil]
            b_Stil = [S.bufs(NT) for _ in range(2)]
            Sst = [A.f32(256) for _ in range(2)]; b_S = S.bufs(2)
            attm = [A.bf(256) for _ in range(2)]; b_attm = S.bufs(2)
            ogt = [A.bf(256) for _ in range(2)]; b_ogt = S.bufs(2)
            junk2 = A.f32(256); b_junk2 = S.buf()
            ss2 = [A.f32(1) for _ in range(2)]; b_ss2 = S.bufs(2)
            rs2 = [A.f32(1) for _ in range(2)]; b_rs2 = S.bufs(2)
            xblocks = [(CT + 512 * b, 512) for b in range(4)]
            tmpA2 = A.f32(E); b_tA2 = S.buf()
            pbuf = [tmpA, tmpA2]; b_pbuf = [b_tA, b_tA2]
            Sst2 = [A.f32(256) for _ in range(2)]
            attm3 = A.bf(256); b_attm3 = S.buf()
            attmR = attm + [attm3]; b_attmR = b_attm + [b_attm3]
            ss2c = A.f32(1); rs2c = A.f32(1)
            ss2R = ss2 + [ss2c]; rs2R = rs2 + [rs2c]
            b_ss2R = b_ss2 + [S.buf()]; b_rs2R = b_rs2 + [S.buf()]
            SS = [[Sst[0], Sst2[0]], [Sst[1], Sst2[1]]]
            b_SS = [[S.buf(), S.buf()], [S.buf(), S.buf()]]
            for h in range(4):
                wq3 = wq.rearrange("p (kc n) -> p kc n", n=128)
                wk3 = wk.rearrange("p (kc n) -> p kc n", n=128)
                wv3 = wvv.rearrange("p (kc n) -> p kc n", n=256)
                wg3 = wg.rearrange("p (kc n) -> p kc n", n=256)
                ldc(wg3, wsrc[:, :, 2048 + h * 256:2048 + (h + 1) * 256], b_wg, "wg")
                ldc(wq3, wsrc[:, :, h * 128:(h + 1) * 128], b_wq, "wq")
                ldc(wk3, wsrc[:, :, 512 + h * 128:512 + (h + 1) * 128], b_wk, "wk")
                ldc(wv3, wsrc[:, :, 1024 + h * 256:1024 + (h + 1) * 256], b_wv, "wv")
                tB3 = tmpB.rearrange("p (c t) -> p c t", t=128)

                def gpair(pr):
                    bk = 5 + (pr % 3)
                    for half in range(2):
                        e_ = pr * 2 + half + 2
                        for kc in range(KC):
                            S.op("pe", lambda e, bk=bk, half=half, e_=e_, kc=kc: e.matmul(
                                bank(bk, 256, half * 256), lhsT=hnE3[:, kc, e_ * 128:(e_ + 1) * 128], rhs=wg3[:, kc, :],
                                start=(kc == 0), stop=(kc == KC - 1)), reads=[b_wg, b_hnE[e_]], writes=[pb[bk]])
                    S.op("act", lambda e, bk=bk, pr=pr: e.activation(out=sgh[:, pr * 512:(pr + 1) * 512], in_=bank(bk), func=AF.Silu),
                         reads=[pb[bk]], writes=[b_sgh[2 * pr], b_sgh[2 * pr + 1]])

                def vpair(pr):
                    bk = pr % 5
                    for half in range(2):
                        e_ = pr * 2 + half
                        for kc in range(KC):
                            S.op("pe", lambda e, bk=bk, half=half, e_=e_, kc=kc: e.matmul(
                                bank(bk, 256, half * 256), lhsT=hnE3[:, kc, e_ * 128:(e_ + 1) * 128], rhs=wv3[:, kc, :],
                                start=(kc == 0), stop=(kc == KC - 1)), reads=[b_wv, b_hnE[e_]], writes=[pb[bk]])
                    if pr % 2 == 1:
                        S.op("act", lambda e, bk=bk, pr=pr: e.copy(out=vh[:, pr * 512:(pr + 1) * 512], in_=bank(bk)),
                             reads=[pb[bk]], writes=[b_vh[2 * pr], b_vh[2 * pr + 1]])
                    else:
                        S.op("dve", lambda e, bk=bk, pr=pr: e.tensor_copy(out=vh[:, pr * 512:(pr + 1) * 512], in_=bank(bk)),
                             reads=[pb[bk]], writes=[b_vh[2 * pr], b_vh[2 * pr + 1]])

                def qproj():
                    for bi, (c0, cn) in enumerate(xblocks):
                        for kc in range(KC):
                            S.op("pe", lambda e, bi=bi, c0=c0, cn=cn, kc=kc: e.matmul(
                                bank(bi), lhsT=wq3[:, kc, :], rhs=hnE3[:, kc, c0:c0 + cn], start=(kc == 0), stop=(kc == KC - 1)),
                                reads=[b_wq] + b_hnE[c0 // 128:(c0 + cn) // 128], writes=[pb[bi]])
                        S.op("act", lambda e, bi=bi: e.activation(out=qT[:, bi * 512:(bi + 1) * 512], in_=bank(bi), func=AF.Identity, scale=128.0 ** -0.5),
                             reads=[pb[bi]], writes=[b_qT])

                def kproj():
                    for bi, (c0, cn) in enumerate(eblocks):
                        for kc in range(KC):
                            S.op("pe", lambda e, bi=bi, c0=c0, cn=cn, kc=kc: e.matmul(
                                bank(bi, cn), lhsT=wk3[:, kc, :], rhs=hnE3[:, kc, c0:c0 + cn], start=(kc == 0), stop=(kc == KC - 1)),
                                reads=[b_wk] + b_hnE[c0 // 128:(c0 + cn) // 128], writes=[pb[bi]])
                        S.op("dve", lambda e, bi=bi, c0=c0, cn=cn: e.tensor_copy(out=kT[:, c0:c0 + cn], in_=bank(bi, cn)),
                             reads=[pb[bi]], writes=[b_kT])

                def zmm(d):
                    for bi, (c0, cn) in enumerate(eblocks):
                        S.op("pe", lambda e, bi=bi, c0=c0, cn=cn, d=d, h=h: e.matmul(
                            bank(bi, cn), lhsT=wa2[32 * d:32 * d + 32, h * 128:(h + 1) * 128], rhs=alow[32 * d:32 * d + 32, c0:c0 + cn],
                            start=True, stop=True), reads=[b_wa2, b_alow], writes=[pb[bi]])
                        S.op("act", lambda e, bi=bi, c0=c0, cn=cn, d=d: e.activation(out=pbuf[d][:, c0:c0 + cn], in_=bank(bi, cn), func=AF.Exp, scale=-1.0),
                             reads=[pb[bi]], writes=[b_pbuf[d]])

                def lnp(d):
                    S.op("act", lambda e, d=d: e.activation(out=pbuf[d], in_=pbuf[d], func=AF.Ln, bias=onecol, scale=1.0),
                         reads=[b_pbuf[d], b_onecol], writes=[b_pbuf[d]])

                def scan(d):
                    S.op("dve", lambda e, d=d: e.tensor_tensor_scan(out=tmpB, data0=scanmask, data1=pbuf[d], initial=0.0, op0=ALU.mult, op1=ALU.add),
                         reads=[b_scanmask, b_pbuf[d]], writes=[b_tB])
                    S.op("dve", lambda e, tB3=tB3: e.tensor_copy(out=tots.unsqueeze(2), in_=tB3[:, :, 127:128]),
                         reads=[b_tB], writes=[b_tots])

                def decop(d):
                    S.op("act", lambda e, d=d: e.activation(out=dec[d], in_=tots, func=AF.Exp, scale=-1.0 / 16.0),
                         reads=[b_tots], writes=[b_dec[d]])

                def Rop(d):
                    if d == 0:
                        S.op("dve", lambda e, tB3=tB3: e.tensor_tensor(out=tB3, in0=tots.unsqueeze(2).to_broadcast([128, NE, 128]), in1=tB3, op=ALU.subtract),
                             reads=[b_tots, b_tB], writes=[b_tB])
                    else:
                        S.op("dve", lambda e: e.tensor_tensor(out=tmpB, in0=tmpB, in1=tmpA2, op=ALU.subtract),
                             reads=[b_tB, b_tA2], writes=[b_tB])

                def Eplus():
                    S.op("act", lambda e: e.activation(out=tmpA, in_=tmpB, func=AF.Exp, scale=1.0 / 16.0), reads=[b_tB], writes=[b_tA])

                def Eminus():
                    S.op("act", lambda e: e.activation(out=tmpA, in_=tmpB, func=AF.Exp, scale=-1.0 / 16.0), reads=[b_tB], writes=[b_tA])

                def qsmul(d):
                    S.op("dve", lambda e, d=d: e.tensor_tensor(out=qs[d], in0=qT, in1=tmpA[:, CT:E], op=ALU.mult),
                         reads=[b_qT, b_tA], writes=[b_qs[d]])

                def kdTmul(d):
                    S.op("dve", lambda e, d=d: e.tensor_tensor(out=kdT[d], in0=kT, in1=tmpA, op=ALU.mult),
                         reads=[b_kT, b_tA], writes=[b_kdT[d]])

                def trgroup(d, g4):
                    bk = 5 + (g4 % 3)
                    tiles = list(range(g4 * 4, min(NE, g4 * 4 + 4)))
                    ptb = bank(bk).bitcast(BF16).rearrange("p (c t) -> p c t", t=128)
                    for ti, e_ in enumerate(tiles):
                        S.op("pe", lambda e, ptb=ptb, ti=ti, e_=e_, d=d: e.transpose(out=ptb[:, ti, :], in_=kdT[d][:, e_ * 128:(e_ + 1) * 128], identity=ident_b),
                             reads=[b_kdT[d], b_identb], writes=[pb[bk]])
                    nt_ = len(tiles)
                    if True:
                        S.op("act", lambda e, ptb=ptb, nt_=nt_, g4=g4, d=d: e.copy(out=kd3[d][:, g4 * 4:g4 * 4 + nt_, :], in_=ptb[:, 0:nt_, :]),
                             reads=[pb[bk]], writes=[b_kd[d][e_] for e_ in tiles])
                    else:
                        S.op("dve", lambda e, ptb=ptb, nt_=nt_, g4=g4, d=d: e.tensor_copy(out=kd3[d][:, g4 * 4:g4 * 4 + nt_, :], in_=ptb[:, 0:nt_, :]),
                             reads=[pb[bk]], writes=[b_kd[d][e_] for e_ in tiles])

                for pr in range(NT // 2):
                    gpair(pr)
                zmm(0); zmm(1); lnp(0); lnp(1)
                qproj()
                scan(0); decop(0)
                kproj()
                Rop(0); Eplus(); qsmul(0); Eminus(); kdTmul(0)
                vpair(0); vpair(1)
                scan(1); vpair(2); decop(1); Rop(1); vpair(3); Eplus(); trgroup(0, 0); vpair(4); qsmul(1); Eminus()
                trgroup(0, 1); vpair(5); kdTmul(1); trgroup(0, 2); vpair(6); trgroup(0, 3); vpair(7); trgroup(0, 4); vpair(8)
                for g4 in range(5):
                    trgroup(1, g4)
                if h == 3:
                    S.op("pool", lambda e: e.dma_start(out=wo3, in_=gwout_d.rearrange("(kc p) n -> p kc n", p=128)),
                         writes=[b_qT, b_kT, b_tA, b_tB, b_wo], dma="wo")
                orders = [list(range(NE)), [1, 0] + list(range(NE - 1, 1, -1))]
                for d in range(2):
                    S.op("dve", lambda e, d=d: e.memset(SS[d][0], 0.0), writes=[b_SS[d][0]])
                cnt = 0
                for i in range(NE):
                    for d in range(2):
                        e_ = orders[d][i]
                        bk = cnt % 8; cnt += 1
                        cur, nxt = i % 2, (i + 1) % 2
                        S.op("pe", lambda e, bk=bk, e_=e_, d=d: e.matmul(bank(bk, 256), lhsT=kd3[d][:, e_, :], rhs=vh3[:, e_, :], start=True, stop=True),
                             reads=[b_kd[d][e_], b_vh[e_]], writes=[pb[bk]])
                        if e_ >= 2:
                            S.op("act", lambda e, e_=e_, d=d, cur=cur: e.activation(out=Stil3[d][:, e_ - 2, :], in_=SS[d][cur], func=AF.Identity, scale=dec[d][:, e_:e_ + 1]),
                                 reads=[b_SS[d][cur], b_dec[d]], writes=[b_Stil[d][e_ - 2]])
                        S.op("dve", lambda e, bk=bk, e_=e_, d=d, cur=cur, nxt=nxt: e.scalar_tensor_tensor(
                            out=SS[d][nxt], in0=SS[d][cur], scalar=dec[d][:, e_:e_ + 1], in1=bank(bk, 256), op0=ALU.mult, op1=ALU.add),
                            reads=[b_SS[d][cur], b_dec[d], pb[bk]], writes=[b_SS[d][nxt]])

                def o_s1(t_):
                    e_ = t_ + 2
                    bka = 5 + (t_ % 3)
                    s3 = t_ % 3
                    for d in range(2):
                        S.op("pe", lambda e, bka=bka, d=d, e_=e_, t_=t_: e.matmul(
                            bank(bka, 128, d * 128), lhsT=kdT[d][:, e_ * 128:(e_ + 1) * 128], rhs=qs[d][:, t_ * 128:(t_ + 1) * 128], start=True, stop=True),
                            reads=[b_kdT[d], b_qs[d]], writes=[pb[bka]])
                    S.op("dve", lambda e, bka=bka, s3=s3: e.tensor_tensor(out=attmR[s3], in0=bank(bka, 256), in1=maskUL, op=ALU.mult),
                         reads=[pb[bka], b_mask], writes=[b_attmR[s3]])

                def o_s2(t_):
                    e_ = t_ + 2
                    bko = t_ % 4
                    s3 = t_ % 3
                    for d in range(2):
                        S.op("pe", lambda e, bko=bko, d=d, t_=t_: e.matmul(
                            bank(bko, 256), lhsT=qs[d][:, t_ * 128:(t_ + 1) * 128], rhs=Stil3[d][:, t_, :], start=(d == 0), stop=False),
                            reads=[b_qs[d], b_Stil[d][t_]], writes=[pb[bko]])
                    for d in range(2):
                        S.op("pe", lambda e, bko=bko, d=d, s3=s3, e_=e_: e.matmul(
                            bank(bko, 256), lhsT=attmR[s3][:, d * 128:(d + 1) * 128], rhs=vh3[:, e_, :], start=False, stop=(d == 1)),
                            reads=[b_attmR[s3], b_vh[e_]], writes=[pb[bko]])
                    S.op("act", lambda e, bko=bko, s3=s3: e.activation(out=junk2, in_=bank(bko, 256), func=AF.Square, accum_out=ss2R[s3]),
                         reads=[pb[bko]], writes=[b_junk2, b_ss2R[s3]])
                    rstd_from_ss(ss2R[s3], rs2R[s3], [b_ss2R[s3]], b_rs2R[s3], 1.0 / 256.0)

                def o_s3(t_):
                    bko = t_ % 4
                    s3 = t_ % 3
                    s = t_ % 2
                    S.op("dve", lambda e, bko=bko, s=s, s3=s3, t_=t_: e.scalar_tensor_tensor(
                        out=ogt[s], in0=bank(bko, 256), scalar=rs2R[s3], in1=sgh3[:, t_, :], op0=ALU.mult, op1=ALU.mult),
                        reads=[pb[bko], b_rs2R[s3], b_sgh[t_]], writes=[b_ogt[s]])

                def o_s4(t_):
                    bka = 5 + (t_ % 3)
                    s = t_ % 2
                    ptb = bank(bka).bitcast(BF16)[:, 512:768].rearrange("p (c t) -> p c t", t=128)
                    for cc in range(2):
                        S.op("pe", lambda e, ptb=ptb, cc=cc, s=s: e.transpose(out=ptb[:, cc, :], in_=ogt[s][:, cc * 128:(cc + 1) * 128], identity=ident_b),
                             reads=[b_ogt[s], b_identb], writes=[pb[bka]])
                    S.op("act", lambda e, ptb=ptb, h=h, t_=t_: e.copy(out=ogT3[:, 2 * h:2 * h + 2, t_ * 128:(t_ + 1) * 128], in_=ptb),
                         reads=[pb[bka]], writes=[b_ogT[t_]])

                for s_ in range(NT + 3):
                    if h == 3 and 4 <= s_ < 12:
                        kc_ = s_ - 4
                        S.op("dve", lambda e, kc_=kc_: e.tensor_scalar(out=wo3[:, kc_, :], in0=wo3[:, kc_, :], scalar1=hgT[:, kc_:kc_ + 1], scalar2=None, op0=ALU.mult),
                             reads=[b_wo, b_hgT], writes=[b_wo])
                    if s_ < NT:
                        o_s1(s_)
                    if 0 <= s_ - 1 < NT:
                        o_s2(s_ - 1)
                    if 0 <= s_ - 2 < NT:
                        o_s3(s_ - 2)
                    if 0 <= s_ - 3 < NT:
                        o_s4(s_ - 3)
            dbg_dump(f"ogT{j}", ogT, [128, KC * T], BF16, b_ogT)
            stop_at("A3")

            S.barrier()
            A.release(m_og)
            A.limit = HT_OFF
            NXT = 3
            xt_off = A.top
            xt = [A.f32(D) for _ in range(NXT)]; b_xt = S.bufs(NXT)
            hn3 = ogT3
            b_hn = [[S.buf(), S.buf()] for _ in range(4)]
            fn1, fn2, _ = make_norm(j, Affn, 3, 0, hn3, b_hn, extra_w=[b_ogT[b * 4:b * 4 + 4] for b in range(4)], pbanks=(0, 1))
            ada_dma = ada_mm = ada_fin = None
            if j == 0:
                assert wo_end + 2 * KC * 512 * 2 <= HT_OFF
                adaH = [ar[:, (wo_end + i_ * KC * 512 * 2) // 4: (wo_end + i_ * KC * 512 * 2) // 4 + KC * 512 // 2].bitcast(BF16)
                        for i_ in range(2)]
                ada_dma, ada_mm, ada_fin_part, ada_fin = make_ada1(adaH, S.bufs(2, "adaH"))
                ada_dma(6); ada_dma(7)
            assert A.top <= dead_off
            ada_hp = [6]
            ada_calls = [0]

            def ada_step():
                ada_calls[0] += 1
                if ada_mm is None or ada_hp[0] >= 12 or ada_calls[0] % 2 == 0:
                    return
                hp = ada_hp[0]
                ada_mm(hp); ada_mm(hp + 1)
                ada_dma(hp + 2); ada_dma(hp + 3)
                ada_hp[0] += 2
            for t_ in range(NT):
                s = t_ % NXT
                ld(xt[s], x2[j, t_ * 128:(t_ + 1) * 128, :], b_xt[s], f"xt{s}")
                for hb in range(2):
                    bk = (t_ * 2 + hb) % 4
                    for c4 in range(4):
                        kc = hb * 4 + c4
                        S.op("pe", lambda e, bk=bk, c4=c4, kc=kc, s=s: e.transpose(out=bank(bk, 128, c4 * 128), in_=xt[s][:, kc * 128:(kc + 1) * 128], identity=ident_f),
                             reads=[b_xt[s], b_identf], writes=[pb[bk]])
                    src = bank(bk).rearrange("p (c t) -> p c t", t=128)
                    dst = hT3[:, hb * 4:hb * 4 + 4, t_ * 128:(t_ + 1) * 128]
                    wb = [b_hT[hb * 4 + c4][t_ // 4] for c4 in range(4)]
                    if hb == 0:
                        S.op("act", lambda e, src=src, dst=dst: e.copy(out=dst, in_=src), reads=[pb[bk]], writes=wb)
                    else:
                        S.op("dve", lambda e, src=src, dst=dst: e.tensor_copy(out=dst, in_=src), reads=[pb[bk]], writes=wb)

            b_wupB = [S.bufs(3), S.bufs(3)]
            up_src0 = fup_d[0].rearrange("(kc p) n -> p kc n", p=128)
            for g_ in range(2):
                o_ = xt_off + g_ * KC * 256 * 2
                wvp = ar[:, o_ // 4: o_ // 4 + KC * 256 // 2].bitcast(BF16).rearrange("p (kc n) -> p kc n", n=256)
                S.op("pool", lambda e, wvp=wvp, g_=g_: e.dma_start(out=wvp[:, :, 0:128], in_=up_src0[:, :, g_ * 128:(g_ + 1) * 128]),
                     writes=[b_wupB[0][g_]] + b_xt, dma=f"wupA{g_}")
                S.op("pool", lambda e, wvp=wvp, g_=g_: e.dma_start(out=wvp[:, :, 128:256], in_=up_src0[:, :, FH + g_ * 128:FH + (g_ + 1) * 128]),
                     writes=[b_wupB[1][g_]] + b_xt, dma=f"wupG{g_}")

            def outproj(blk, mcs=range(KC)):
                sl = slice(blk * 512, (blk + 1) * 512)
                for mc in mcs:
                    bk = 4 + (mc % 4)
                    for kc in range(KC):
                        S.op("pe", lambda e, bk=bk, mc=mc, kc=kc, sl=sl: e.matmul(
                            bank(bk), lhsT=wo3[:, kc, mc * 128:(mc + 1) * 128], rhs=ogT3[:, kc, sl], start=(kc == 0), stop=(kc == KC - 1)),
                            reads=[b_wo] + b_ogT[blk * 4:blk * 4 + 4], writes=[pb[bk]])
                    S.op("dve", lambda e, bk=bk, mc=mc, sl=sl: e.scalar_tensor_tensor(
                        out=hT3[:, mc, sl], in0=bank(bk), scalar=modcol(0, 2, mc, j), in1=hT3[:, mc, sl], op0=ALU.mult, op1=ALU.add),
                        reads=[pb[bk], b_mod, b_hT[mc][blk]], writes=[b_hT[mc][blk]])

            ada_step()
            for blk in range(4):
                outproj(blk, range(0, 4))
                if blk >= 1:
                    fn1(blk - 1); fn2(blk - 1)
                outproj(blk, range(4, KC))
                ada_step()
            fn1(3); fn2(3)
            ada_step()
            if ada_fin is not None:
                assert ada_hp[0] == 12, ada_hp[0]
                ada_fin_part(24, 48)
                ada_fin()
            dbg_dump(f"hA{j}", hT, [128, KC * T], F32, [b for r in b_hT for b in r])
            stop_at("A4")

            S.barrier()
            A.release(base_mark)
            hn = A.bf(KC * T)
            assert hn.offset == ogT.offset
            conv_ffn(j, 0, hn3, b_hn, pre={"tok": b_wupB, "off": xt_off}, tail_norm=(j, Amix, 0, 1, hn3, b_hn))
            dbg_dump(f"hB{j}", hT, [128, KC * T], F32, [b for r in b_hT for b in r])
            stop_at("B")

            S.barrier()
            mC = A.mark()
            wsc_off = A.top
            wsc = [A.bf(KC * 384) for _ in range(2)]; b_wscP = [S.bufs(2), S.bufs(2), S.bufs(2)]
            wo = A.bf(KC * D); wo3 = wo.rearrange("p (kc n) -> p kc n", n=D); b_wo = S.buf()
            yp_off = A.top
            yp = A.bf(KC * T); yp3 = yp.rearrange("p (kc t) -> p kc t", t=T); b_yp = S.bufs(KC)
            scsrc = scwin_d.rearrange("(kc p) n -> p kc n", p=128)

            def load_sc(c):
                if c >= KC:
                    return
                s_ = c % 2
                w3_ = wsc[s_].rearrange("p (kc n) -> p kc n", n=384)
                for part in range(3):
                    ldc(w3_[:, :, part * 128:(part + 1) * 128], scsrc[:, :, part * D + c * 128: part * D + (c + 1) * 128], b_wscP[part][s_], f"wsc{part}_{s_}")

            def c_loads():
                load_sc(0); load_sc(1)
                ldc(wo3, scwout_d.rearrange("(kc p) n -> p kc n", p=128), b_wo, "wo2")
            ntokC = run_norm((j, Amix, 0, 1, hn3, b_hn), sub_arena(yp_off, KC * T * 2), after_first=c_loads, blocks=(2, 3))
            stop_at("Cnorm")
            c_off = A.top
            Csb = A.f32(T); b_C = S.buf()
            cv = A.f32(T); b_cv = S.buf()
            co = A.f32(T); b_co = S.buf()
            scw3 = scw.rearrange("p (c k) -> p c k", k=3)
            cv3 = cv.rearrange("p (r w) -> p r w", w=64)
            co3 = co.rearrange("p (r w) -> p r w", w=64)
            pgi = 0
            for c in range(KC):
                s = c % 2
                w3 = wsc[s].rearrange("p (kc n) -> p kc n", n=384)
                for part in (1, 2, 0):
                    pg = pgi % 2; pgi += 1
                    for blk in range(4):
                        for kc in range(KC):
                            S.op("pe", lambda e, w3=w3, part=part, pg=pg, blk=blk, kc=kc: e.matmul(
                                bank(pg * 4 + blk), lhsT=w3[:, kc, part * 128:(part + 1) * 128], rhs=hn3[:, kc, blk * 512:(blk + 1) * 512],
                                start=(kc == 0), stop=(kc == KC - 1)), reads=[b_wscP[part][s]] + b_hn[blk], writes=[pb[pg * 4 + blk]])
                    pall = pst[:, pg * 2048:(pg + 1) * 2048]
                    rb = pb[pg * 4:pg * 4 + 4]
                    if part == 1:
                        S.op("act", lambda e, pall=pall: e.copy(out=Csb, in_=pall), reads=rb, writes=[b_C])
                    elif part == 2:
                        S.op("dve", lambda e, pall=pall: e.tensor_tensor(out=cv, in0=pall, in1=Csb, op=ALU.mult), reads=rb + [b_C], writes=[b_cv])
                        S.op("act", lambda e, c=c: e.activation(out=co, in_=cv, func=AF.Identity, scale=scw3[:, c, 1:2]), reads=[b_cv, b_scw], writes=[b_co])
                        S.op("dve", lambda e, c=c: e.scalar_tensor_tensor(out=co3[:, :, 1:64], in0=cv3[:, :, 0:63], scalar=scw3[:, c, 0:1], in1=co3[:, :, 1:64],
                                                                          op0=ALU.mult, op1=ALU.add), reads=[b_cv, b_scw, b_co], writes=[b_co])
                        S.op("dve", lambda e, c=c: e.scalar_tensor_tensor(out=co3[:, :, 0:63], in0=cv3[:, :, 1:64], scalar=scw3[:, c, 2:3], in1=co3[:, :, 0:63],
                                                                          op0=ALU.mult, op1=ALU.add), reads=[b_cv, b_scw, b_co], writes=[b_co])
                    else:
                        S.op("dve", lambda e, pall=pall, c=c: e.tensor_tensor(out=yp3[:, c, :], in0=pall, in1=co, op=ALU.mult), reads=rb + [b_co], writes=[b_yp[c]] + ntokC)
                        load_sc(c + 2)
            dead_tok = [t_ for p_ in b_wscP for t_ in p_] + [b_C, b_cv, b_co]
            dn1, dn2, _ = make_norm(j, Affn, 3, 1, hn3, b_hn, pbanks=(0, 1),
                                    AL=ChainArena([sub_arena(wsc_off, 2 * KC * 384 * 2), sub_arena(c_off, 3 * T * 4)]), pre_w=dead_tok)
            oc = [0]

            def c_outproj(blk, mcs=range(KC)):
                sl = slice(blk * 512, (blk + 1) * 512)
                for mc in mcs:
                    bk = 2 + (oc[0] % 6); oc[0] += 1
                    for kc in range(KC):
                        S.op("pe", lambda e, bk=bk, mc=mc, kc=kc, sl=sl: e.matmul(
                            bank(bk), lhsT=wo3[:, kc, mc * 128:(mc + 1) * 128], rhs=yp3[:, kc, sl], start=(kc == 0), stop=(kc == KC - 1)),
                            reads=[b_wo, b_yp[kc]], writes=[pb[bk]])
                    S.op("dve", lambda e, bk=bk, mc=mc, sl=sl: e.scalar_tensor_tensor(
                        out=hT3[:, mc, sl], in0=bank(bk), scalar=modcol(1, 2, mc, j), in1=hT3[:, mc, sl], op0=ALU.mult, op1=ALU.add),
                        reads=[pb[bk], b_mod, b_hT[mc][blk]], writes=[b_hT[mc][blk]])

            for blk in range(4):
                c_outproj(blk, range(0, 4))
                if blk >= 1:
                    dn1(blk - 1); dn2(blk - 1)
                c_outproj(blk, range(4, KC))
            dn1(3); dn2(3)
            A.release(mC)
            dbg_dump(f"hC{j}", hT, [128, KC * T], F32, [b for r in b_hT for b in r])
            stop_at("C")

            S.barrier()
            conv_ffn(j, 1, hn3, b_hn)
            stop_at("D")

            S.barrier()
            A.release(base_mark)
            NOT = 4
            ot = [A.f32(D) for _ in range(NOT)]; b_ot = S.bufs(NOT)
            junk3 = A.bf(D); b_junk3 = S.buf()
            ss3 = [A.f32(1) for _ in range(3)]; b_ss3 = S.bufs(3)
            rs3 = [A.f32(1) for _ in range(3)]; b_rs3 = S.bufs(3)

            def e_s1(t_):
                pg = t_ % 4
                for kc in range(KC):
                    bk = pg * 2 + kc // 4
                    S.op("pe", lambda e, bk=bk, kc=kc, t_=t_: e.transpose(out=bank(bk, 128, (kc % 4) * 128), in_=hT3[:, kc, t_ * 128:(t_ + 1) * 128], identity=ident_f),
                         reads=[b_hT[kc][t_ // 4], b_identf], writes=[pb[bk]])

            def e_s2(t_):
                pg = t_ % 4
                s3 = t_ % 3
                pall = pst[:, pg * 1024:(pg + 1) * 1024]
                S.op("act", lambda e, pall=pall, s3=s3: e.activation(out=junk3, in_=pall, func=AF.Square, accum_out=ss3[s3]),
                     reads=[pb[pg * 2], pb[pg * 2 + 1]], writes=[b_junk3, b_ss3[s3]])
                rstd_from_ss(ss3[s3], rs3[s3], [b_ss3[s3]], b_rs3[s3], 1.0 / D)

            def e_s3(t_):
                pg = t_ % 4
                s3 = t_ % 3
                s = t_ % NOT
                pall = pst[:, pg * 1024:(pg + 1) * 1024]
                S.op("dve", lambda e, pall=pall, s=s, s3=s3: e.scalar_tensor_tensor(out=ot[s], in0=pall, scalar=rs3[s3], in1=fnorm, op0=ALU.mult, op1=ALU.mult),
                     reads=[pb[pg * 2], pb[pg * 2 + 1], b_rs3[s3], b_fnorm], writes=[b_ot[s]])
                S.op("sp", lambda e, s=s, t_=t_: e.dma_start(out=out2[j, t_ * 128:(t_ + 1) * 128, :], in_=ot[s]), reads=[b_ot[s]], dma=f"ot{s}")

            for s_ in range(NT + 2):
                if s_ < NT:
                    e_s1(s_)
                if 0 <= s_ - 1 < NT:
                    e_s2(s_ - 1)
                if 0 <= s_ - 2 < NT:
                    e_s3(s_ - 2)
            stop_at("E")
        except _Stop:
            pass
        S.barrier()
        counts = S.emit(st)
        print("op counts", counts, flush=True)
    return nc, dbg


def prep_inputs(inp):
    f = lambda a: np.ascontiguousarray(np.asarray(a, dtype=np.float32))
    x = f(inp["x"]); c = f(inp["c"]); ctx = f(inp["ctx"]); c_ctx = f(inp["c_ctx"])
    pm = lambda v, n: np.ascontiguousarray(v.reshape(n, 128).T)
    shared = {}
    shared["ada_w"] = f(inp["ada_w"])
    ada_b = f(inp["ada_b"])
    shared["adab"] = np.stack([pm(ada_b[l], 48) for l in range(2)])
    shared["nmix"] = np.stack([pm(f(inp["norm_mix"])[l], 8) for l in range(2)])
    shared["nffn"] = np.stack([pm(f(inp["norm_ffn"])[l], 8) for l in range(2)])
    shared["fnorm"] = np.ascontiguousarray(np.broadcast_to(f(inp["final_norm"])[None, :], (128, D)))
    gw = f(inp["gla_w_in"])[0]
    shared["gla_w_in"] = gw
    wap = np.zeros((D, 64), np.float32)
    wap[:, 0:16] = gw[:, 3072:3088]
    wap[:, 32:48] = gw[:, 3088:3104]
    shared["wa_pad"] = wap
    wa2 = np.zeros((64, 512), np.float32)
    w_a2 = f(inp["gla_w_a2"])[0]; b_a = f(inp["gla_b_a"])[0]
    wa2[0:16] = w_a2[0]; wa2[16] = b_a[0]
    wa2[32:48] = w_a2[1]; wa2[48] = b_a[1]
    shared["wa2aug"] = wa2
    hg = f(inp["gla_head_norm"])[0]
    shared["hgT"] = np.ascontiguousarray(np.tile(hg.reshape(2, 128).T, (1, 4)))
    shared["gla_w_out"] = f(inp["gla_w_out"])[0]
    shared["sc_w_in"] = f(inp["sc_w_in"])[0]
    scw = f(inp["sc_conv_w"])[0]
    shared["scw"] = np.ascontiguousarray(scw.reshape(3, 8, 128).transpose(2, 1, 0).reshape(128, 24))
    shared["sc_w_out"] = f(inp["sc_w_out"])[0]
    shared["ffn_w_up"] = f(inp["ffn_w_up"])
    fcw = f(inp["ffn_conv_w"])
    shared["fcw"] = np.ascontiguousarray(fcw.reshape(2, 3, 40, 128).transpose(0, 3, 2, 1).reshape(2, 128, 120))
    fcb = f(inp["ffn_conv_b"])
    shared["fcb"] = np.ascontiguousarray(fcb.reshape(2, 40, 128).transpose(0, 2, 1))
    shared["ffn_w_down"] = f(inp["ffn_w_down"])
    shared["ident"] = np.eye(128, dtype=np.float32)
    shared["ones"] = np.ones((128, 128), np.float32)
    jj, ii = np.meshgrid(np.arange(128), np.arange(128), indexing="ij")
    shared["maskUL"] = np.concatenate([(jj <= ii), (jj >= ii)], axis=1).astype(np.float32)
    in_maps = []
    for core in range(NCORES):
        m = dict(shared)
        m["x2"] = x[2 * core:2 * core + 2]
        m["ctx2"] = ctx[2 * core:2 * core + 2]
        cc = np.stack([c[2 * core], c[2 * core + 1], c_ctx], axis=0)
        m["cT"] = np.ascontiguousarray(cc.reshape(3, 8, 128).transpose(2, 1, 0).reshape(128, 24))
        in_maps.append(m)
    return in_maps


_NC_CACHE = {}


def kernel(**inputs):
    in_maps = prep_inputs(inputs)
    if "nc" not in _NC_CACHE:
        _NC_CACHE["nc"] = build_nc(False)[0]
    res = run_bass_kernel_spmd(_NC_CACHE["nc"], in_maps, core_ids=list(range(NCORES)))
    out = np.concatenate([np.asarray(r["out2"]) for r in res.results], axis=0)
    return out.astype(np.float32)
```

```python
import types
import numpy as np
from contextlib import ExitStack
import concourse.bass as bass
import concourse.mybir as mybir
from concourse.bass_utils import run_bass_kernel_spmd

F32 = mybir.dt.float32
BF16 = mybir.dt.bfloat16
AF = mybir.ActivationFunctionType
ALU = mybir.AluOpType
AX = mybir.AxisListType

NCORES = 8
D = 1024
T = 2048
CT = 256
E = T + CT
NT = T // 128
NE = E // 128
KC = 8
FH = 2560
EPS = 1e-6
STOP = None
MARKS = []


def _freeze(fn):
    if fn is None or fn.__closure__ is None:
        return fn
    cells = []
    for c in fn.__closure__:
        try:
            cells.append(types.CellType(c.cell_contents))
        except ValueError:
            cells.append(c)
    return types.FunctionType(fn.__code__, fn.__globals__, fn.__name__, fn.__defaults__, tuple(cells))


class Buf:
    __slots__ = ("name", "lw", "rd")

    def __init__(self, name=""):
        self.name = name
        self.lw = None
        self.rd = []


class Sched:
    ENGS = ("pe", "act", "dve", "pool", "sp")

    def __init__(self, nc):
        self.nc = nc
        self.ops = {e: [] for e in self.ENGS}
        self.waited = {e: {} for e in self.ENGS}
        self.dma_cnt = {}

    def buf(self, name=""):
        return Buf(name)

    def bufs(self, n, name=""):
        return [Buf(f"{name}{i}") for i in range(n)]

    @staticmethod
    def _add_dep(deps, ref):
        if ref is None:
            return
        k, v = ref
        if deps.get(k, -1) < v:
            deps[k] = v

    def op(self, eng, fn, reads=(), writes=(), dma=None):
        fn = _freeze(fn)
        deps = {}
        for b in reads:
            self._add_dep(deps, b.lw)
        for b in writes:
            if b.lw is not None and (b.lw[0] != eng or dma is not None):
                self._add_dep(deps, b.lw)
            for r in b.rd:
                if r[0] != eng or dma is not None:
                    self._add_dep(deps, r)
        idx = len(self.ops[eng])
        waits = []
        wd = self.waited[eng]
        for k, v in deps.items():
            if k == eng and (eng == "pe" or dma is not None):
                continue
            if wd.get(k, -1) >= v:
                continue
            wd[k] = v
            waits.append((k, v))
        if dma is not None:
            c = self.dma_cnt.get(dma, 0)
            self.dma_cnt[dma] = c + 1
            ref = (("dma", dma), c)
        else:
            ref = (eng, idx)
        self.ops[eng].append({"fn": fn, "waits": waits, "dma": dma, "sig": False})
        for b in reads:
            b.rd.append(ref)
        for b in writes:
            b.lw = ref
            b.rd = []
        return ref

    def barrier(self):
        last = {}
        for e in self.ENGS:
            for i in range(len(self.ops[e]) - 1, -1, -1):
                if self.ops[e][i]["dma"] is None and self.ops[e][i]["fn"] is not None:
                    last[e] = i
                    break
        dl = {("dma", c): n - 1 for c, n in self.dma_cnt.items()}
        for e in self.ENGS:
            waits = []
            wd = self.waited[e]
            for k, v in list(last.items()) + list(dl.items()):
                if k == e:
                    continue
                if wd.get(k, -1) >= v:
                    continue
                wd[k] = v
                waits.append((k, v))
            if waits:
                self.ops[e].append({"fn": None, "waits": waits, "dma": None, "sig": False})

    def emit(self, stack):
        nc = self.nc
        for e in self.ENGS:
            for o in self.ops[e]:
                for (k, v) in o["waits"]:
                    if isinstance(k, str):
                        self.ops[k][v]["sig"] = True
        sigidx = {}
        for e in self.ENGS:
            c = 0
            arr = []
            for o in self.ops[e]:
                if o["sig"]:
                    c += 1
                arr.append(c)
            sigidx[e] = arr
        sems = {}
        for e in self.ENGS:
            sems[e] = stack.enter_context(nc.semaphore(f"s_{e}"))
        for c in self.dma_cnt:
            sems[("dma", c)] = stack.enter_context(nc.semaphore(f"d_{c}"))
        block = stack.enter_context(nc.Block())

        def run(e):
            def body(eng):
                for o in self.ops[e]:
                    for (k, v) in o["waits"]:
                        if isinstance(k, str):
                            eng.wait_ge(sems[k], sigidx[k][v])
                        else:
                            eng.wait_ge(sems[k], 16 * (v + 1))
                    if o["fn"] is None:
                        continue
                    ins = o["fn"](eng)
                    if o["dma"] is not None:
                        ins.then_inc(sems[("dma", o["dma"])], 16)
                    elif o["sig"]:
                        ins.then_inc(sems[e], 1)
            return body

        block.tensor(run("pe"))
        block.scalar(run("act"))
        block.vector(run("dve"))
        block.gpsimd(run("pool"))
        block.sync(run("sp"))
        return {e: len(self.ops[e]) for e in self.ENGS}


class Arena:
    def __init__(self, ar, nbytes):
        self.ar = ar
        self.n = nbytes
        self.limit = nbytes
        self.top = 0
        self.S = None

    def alloc(self, nbytes):
        nbytes = (nbytes + 63) // 64 * 64
        off = self.top
        self.top += nbytes
        assert self.top <= self.limit, f"arena overflow {self.top} > {self.limit}"
        return off

    def f32(self, n):
        off = self.alloc(n * 4)
        return self.ar[:, off // 4: off // 4 + n]

    def bf(self, n):
        off = self.alloc(n * 2)
        return self.ar[:, off // 4: off // 4 + n // 2].bitcast(BF16)

    def mark(self):
        return self.top

    def release(self, m):
        if self.S is not None:
            self.S.barrier()
        self.top = m


class ChainArena:
    def __init__(self, arenas):
        self.arenas = arenas

    def _pick(self, nbytes):
        nb = (nbytes + 63) // 64 * 64
        for a in self.arenas:
            if a.top + nb <= a.limit:
                return a
        raise AssertionError("chain arena overflow")

    def f32(self, n):
        return self._pick(n * 4).f32(n)

    def bf(self, n):
        return self._pick(n * 2).bf(n)


def build_nc(debug=False):
    nc = bass.Bass("TRN2", target_bir_lowering=False)

    def din(name, shape):
        return nc.dram_tensor(name, list(shape), F32, kind="ExternalInput").ap()

    x2 = din("x2", [2, T, D])
    ctx2 = din("ctx2", [2, CT, D])
    cT_d = din("cT", [128, KC * 3])
    adaw_d = din("ada_w", [2, D, 6 * D])
    adab_d = din("adab", [2, 128, 48])
    nmix_d = din("nmix", [2, 128, KC])
    nffn_d = din("nffn", [2, 128, KC])
    fnorm_d = din("fnorm", [128, D])
    gwin_d = din("gla_w_in", [D, 3104])
    wapad_d = din("wa_pad", [D, 64])
    wa2_d = din("wa2aug", [64, 512])
    hgT_d = din("hgT", [128, KC])
    gwout_d = din("gla_w_out", [D, D])
    scwin_d = din("sc_w_in", [D, 3 * D])
    scw_d = din("scw", [128, KC * 3])
    scwout_d = din("sc_w_out", [D, D])
    fup_d = din("ffn_w_up", [2, D, 2 * FH])
    fcw_d = din("fcw", [2, 128, 40 * 3])
    fcb_d = din("fcb", [2, 128, 40])
    fdn_d = din("ffn_w_down", [2, FH, D])
    ident_d = din("ident", [128, 128])
    ones_d = din("ones", [128, 128])
    mask_d = din("maskUL", [128, 256])
    out2 = nc.dram_tensor("out2", [2, T, D], F32, kind="ExternalOutput").ap()
    dbg = {}

    with ExitStack() as st:
        S = Sched(nc)
        ARN = 52736
        ar = st.enter_context(nc.sbuf_tensor("arena", [128, ARN], F32))
        pst = st.enter_context(nc.psum_tensor("pst", [128, 4096], F32))
        A = Arena(ar, ARN * 4)
        A.S = S
        HT_OFF = ARN * 4 - KC * T * 4

        def bank(b, n=512, off=0):
            return pst[:, b * 512 + off: b * 512 + off + n]

        pb = S.bufs(8, "psb")

        def dbg_dump(name, ap, shape, dt, rbufs):
            if not debug:
                return
            t = nc.dram_tensor("dbg_" + name, list(shape), dt, kind="ExternalOutput").ap()
            dbg[name] = t
            S.op("sp", lambda e: e.dma_start(out=t, in_=ap), reads=rbufs, dma="dbg_" + name)

        ident_f = A.f32(128); b_identf = S.buf()
        ident_b = A.bf(128); b_identb = S.buf()
        ones_b = A.bf(128); b_ones = S.buf()
        maskUL = A.f32(256); b_mask = S.buf()
        scanmask = A.bf(E); b_scanmask = S.buf()
        epsT = A.f32(1); b_eps = S.buf()
        onecol = A.f32(1); b_onecol = S.buf()
        cT = A.f32(KC * 3); b_cT = S.buf()
        scT = A.bf(KC * 4); b_scT = S.buf()
        adab = [A.f32(48) for _ in range(2)]; b_adab = S.buf()
        nmix = [A.f32(KC) for _ in range(2)]
        nffn = [A.f32(KC) for _ in range(2)]; b_gains = S.buf()
        fnorm = A.f32(D); b_fnorm = S.buf()
        hgT = A.f32(KC); b_hgT = S.buf()
        scw = A.f32(KC * 3); b_scw = S.buf()
        fcw = [A.f32(120) for _ in range(2)]
        fcb = [A.f32(40) for _ in range(2)]; b_fc = S.buf()
        wa2 = A.bf(512); b_wa2 = S.buf()
        mod = [A.f32(48 * 3) for _ in range(2)]; b_mod = S.buf()
        Amix = [A.f32(KC * 3) for _ in range(2)]
        Affn = [A.f32(KC * 3) for _ in range(2)]; b_Ader = S.buf()

        _ucls = [0]

        def _cls(c):
            if c in ("c0", "cc0"):
                _ucls[0] += 1
                return f"{c}_{_ucls[0]}"
            return c
        ld = lambda out, in_, wb, cls="c0": S.op("sp", lambda e: e.dma_start(out=out, in_=in_), writes=[wb], dma=_cls(cls))
        ldc = lambda out, in_, wb, cls: S.op("pool", lambda e: e.dma_start(out=out, in_=in_), writes=[wb], dma=_cls(cls))
        ld(ident_f, ident_d, b_identf)
        ld(maskUL, mask_d, b_mask)
        ld(cT, cT_d, b_cT)
        for l in range(2):
            ld(adab[l], adab_d[l], b_adab)
            ld(nmix[l], nmix_d[l], b_gains)
            ld(nffn[l], nffn_d[l], b_gains)
            ld(fcw[l], fcw_d[l], b_fc)
            ld(fcb[l], fcb_d[l], b_fc)
        ld(fnorm, fnorm_d, b_fnorm)
        ld(hgT, hgT_d, b_hgT)
        ld(scw, scw_d, b_scw)
        ldc(ident_b, ident_d, b_identb, "cc0")
        ldc(ones_b, ones_d, b_ones, "cc0")
        ldc(wa2[0:64, :], wa2_d, b_wa2, "cc0")
        S.op("dve", lambda e: e.memset(scanmask, 1.0), writes=[b_scanmask])
        smv = scanmask.rearrange("p (c t) -> p c t", t=128)
        S.op("dve", lambda e: e.memset(smv[:, :, 0:1], 0.0), writes=[b_scanmask])
        S.op("dve", lambda e: e.memset(epsT, EPS), writes=[b_eps])
        S.op("dve", lambda e: e.memset(onecol, 1.0), writes=[b_onecol])

        m0 = A.mark()
        S.op("act", lambda e: e.activation(out=scT.rearrange("p (k j) -> p k j", j=4)[:, :, 0:3],
                                           in_=cT.rearrange("p (k j) -> p k j", j=3), func=AF.Silu),
             reads=[b_cT], writes=[b_scT])
        scT3 = scT.rearrange("p (k j) -> p k j", j=4)
        adaW = [A.bf(KC * D) for _ in range(2)]
        b_adaW = S.bufs(2, "adaW")

        def ada_finish(l, pm, pbtok):
            modv = mod[l].rearrange("p (c j) -> p c j", j=3)
            S.op("dve", lambda e, pm=pm, modv=modv, l=l: e.tensor_tensor(
                out=modv, in0=pm[:, :, 0:3], in1=adab[l].unsqueeze(2).to_broadcast([128, 48, 3]), op=ALU.add),
                reads=[pbtok, b_adab], writes=[b_mod])
            for (dst, lo, gain) in ((Amix[l], 8, nmix[l]), (Affn[l], 32, nffn[l])):
                dv = dst.rearrange("p (c j) -> p c j", j=3)
                S.op("dve", lambda e, dv=dv, modv=modv, lo=lo, gain=gain: e.scalar_tensor_tensor(
                    out=dv, in0=modv[:, lo:lo + 8, :], scalar=1.0, in1=gain.unsqueeze(2).to_broadcast([128, 8, 3]),
                    op0=ALU.add, op1=ALU.mult), reads=[b_mod, b_gains], writes=[b_Ader])

        for l in range(1):
            pm = bank(l, 192).rearrange("p (c j) -> p c j", j=4)
            for k in range(6):
                s = (l * 6 + k) % 2
                wv = adaW[s].rearrange("p (kc n) -> p kc n", n=D)
                src = adaw_d[l].rearrange("(kc p) n -> p kc n", p=128)[:, :, k * D:(k + 1) * D]
                ldc(wv, src, b_adaW[s], f"adaW{s}")
                for mc in range(KC):
                    for kc in range(KC):
                        S.op("pe", lambda e, wv=wv, mc=mc, kc=kc, pm=pm, k=k: e.matmul(
                            pm[:, k * 8 + mc, 0:3], lhsT=wv[:, kc, mc * 128:(mc + 1) * 128], rhs=scT3[:, kc, 0:3],
                            start=(kc == 0), stop=(kc == KC - 1)),
                            reads=[b_adaW[s], b_scT], writes=[pb[l]])
            ada_finish(l, pm, pb[l])

        def make_ada1(adaH, b_adaH):
            pm1 = bank(3, 192).rearrange("p (c j) -> p c j", j=4)
            src1 = adaw_d[1].rearrange("(kc p) n -> p kc n", p=128)

            def dma(hp):
                if hp >= 12:
                    return
                k, half = hp // 2, hp % 2
                s_ = hp % 2
                wv = adaH[s_].rearrange("p (kc n) -> p kc n", n=512)
                ldc(wv, src1[:, :, k * D + half * 512:k * D + half * 512 + 512], b_adaH[s_], f"adaH{s_}")

            def mm(hp):
                k, half = hp // 2, hp % 2
                s_ = hp % 2
                wv = adaH[s_].rearrange("p (kc n) -> p kc n", n=512)
                for m4 in range(4):
                    mc = half * 4 + m4
                    for kc in range(KC):
                        S.op("pe", lambda e, wv=wv, m4=m4, mc=mc, kc=kc, k=k: e.matmul(
                            pm1[:, k * 8 + mc, 0:3], lhsT=wv[:, kc, m4 * 128:(m4 + 1) * 128], rhs=scT3[:, kc, 0:3],
                            start=(kc == 0), stop=(kc == KC - 1)),
                            reads=[b_adaH[s_], b_scT], writes=[pb[3]])

            def fin_part(c0, c1):
                modv = mod[1].rearrange("p (c j) -> p c j", j=3)
                S.op("dve", lambda e, modv=modv, c0=c0, c1=c1: e.tensor_tensor(
                    out=modv[:, c0:c1, :], in0=pm1[:, c0:c1, 0:3], in1=adab[1][:, c0:c1].unsqueeze(2).to_broadcast([128, c1 - c0, 3]), op=ALU.add),
                    reads=[pb[3], b_adab], writes=[b_mod])

            def fin_derived():
                modv = mod[1].rearrange("p (c j) -> p c j", j=3)
                for (dst, lo, gain) in ((Amix[1], 8, nmix[1]), (Affn[1], 32, nffn[1])):
                    dv = dst.rearrange("p (c j) -> p c j", j=3)
                    S.op("dve", lambda e, dv=dv, modv=modv, lo=lo, gain=gain: e.scalar_tensor_tensor(
                        out=dv, in0=modv[:, lo:lo + 8, :], scalar=1.0, in1=gain.unsqueeze(2).to_broadcast([128, 8, 3]),
                        op0=ALU.add, op1=ALU.mult), reads=[b_mod, b_gains], writes=[b_Ader])
            return dma, mm, fin_part, fin_derived

        A.release(m0)
        S.barrier()
        dbg_dump("mod0", mod[0], [128, 144], F32, [b_mod])
        dbg_dump("Amix0", Amix[0], [128, 24], F32, [b_Ader])

        def modcol(l, k, kc, j):
            return mod[l].rearrange("p (c j) -> p c j", j=3)[:, k * 8 + kc, j:j + 1]

        def acol(tbl, l, kc, j):
            return tbl[l].rearrange("p (c j) -> p c j", j=3)[:, kc, j:j + 1]

        hT = ar[:, HT_OFF // 4: HT_OFF // 4 + KC * T]
        A.limit = HT_OFF
        hT3 = hT.rearrange("p (kc t) -> p kc t", t=T)
        b_hT = [[S.buf() for _ in range(4)] for _ in range(KC)]
        base_mark = A.mark()

        def rstd_from_ss(ss_in, out, rb, wb, n_inv):
            S.op("act", lambda e: e.activation(out=out, in_=ss_in, func=AF.Ln, bias=epsT, scale=n_inv),
                 reads=rb + [b_eps], writes=[wb])
            S.op("act", lambda e: e.activation(out=out, in_=out, func=AF.Exp, scale=-0.5), reads=[wb], writes=[wb])

        def make_norm(j, Atbl, shift_k, l, hn3, b_hn, extra_w=None, pbanks=(0, 1), AL=None, pre_w=None):
            AL = AL if AL is not None else A
            sq = AL.bf(KC * 512); b_sq = S.bufs(2)
            sq3 = sq.rearrange("p (kc t) -> p kc t", t=512)
            rs = [AL.f32(512) for _ in range(2)]; b_rs = S.bufs(2)
            tmp = AL.f32(KC * 512); b_tmp = S.buf()
            tmp3 = tmp.rearrange("p (kc t) -> p kc t", t=512)
            n_tokens = b_sq + b_rs + [b_tmp]
            pre_left = {"act": True, "pool": True, "dve": True}

            def prew(eng):
                if pre_w is None or not pre_left[eng]:
                    return []
                pre_left[eng] = False
                return list(pre_w)

            def n1(blk):
                s = blk % 2
                sl = slice(blk * 512, (blk + 1) * 512)
                S.op("act", lambda e, sl=sl: e.activation(out=sq3[:, 0:5, :], in_=hT3[:, 0:5, sl], func=AF.Square),
                     reads=[b_hT[kc][blk] for kc in range(0, 5)], writes=[b_sq[0]] + prew("act"))
                S.op("pool", lambda e, sl=sl: e.tensor_tensor(out=sq3[:, 5:8, :], in0=hT3[:, 5:8, sl], in1=hT3[:, 5:8, sl], op=ALU.mult),
                     reads=[b_hT[kc][blk] for kc in range(5, 8)], writes=[b_sq[1]] + prew("pool"))
                bk = pbanks[s]
                for kc in range(KC):
                    S.op("pe", lambda e, kc=kc, bk=bk: e.matmul(bank(bk), lhsT=ones_b, rhs=sq3[:, kc, :],
                                                               start=(kc == 0), stop=(kc == KC - 1)),
                         reads=[b_sq[0 if kc < 5 else 1], b_ones], writes=[pb[bk]])
                rstd_from_ss(bank(bk), rs[s], [pb[bk]], b_rs[s], 1.0 / D)

            def n2(blk):
                s = blk % 2
                sl = slice(blk * 512, (blk + 1) * 512)
                S.op("dve", lambda e, sl=sl, s=s: e.tensor_tensor(
                    out=tmp3, in0=hT3[:, :, sl], in1=rs[s].unsqueeze(1).to_broadcast([128, KC, 512]), op=ALU.mult),
                    reads=[b_hT[kc][blk] for kc in range(KC)] + [b_rs[s]], writes=[b_tmp] + prew("dve"))
                first = {"act": True, "dve": True}
                for kc in range(KC):
                    eng = "dve" if kc in (3, 7) else "act"
                    wl = [b_hn[blk][0 if eng == "act" else 1]] + (extra_w[blk] if (extra_w is not None and first[eng]) else [])
                    first[eng] = False
                    if eng == "act":
                        S.op("act", lambda e, kc=kc, sl=sl: e.activation(
                            out=hn3[:, kc, sl], in_=tmp3[:, kc, :], func=AF.Identity,
                            scale=acol(Atbl, l, kc, j), bias=modcol(l, shift_k, kc, j)),
                            reads=[b_tmp, b_Ader, b_mod], writes=wl)
                    else:
                        S.op("dve", lambda e, kc=kc, sl=sl: e.tensor_scalar(
                            out=hn3[:, kc, sl], in0=tmp3[:, kc, :], scalar1=acol(Atbl, l, kc, j), scalar2=modcol(l, shift_k, kc, j),
                            op0=ALU.mult, op1=ALU.add), reads=[b_tmp, b_Ader, b_mod], writes=wl)
            return n1, n2, n_tokens

        def sub_arena(ap_bytes_off, nbytes):
            a2 = Arena(ar, A.n)
            a2.top = ap_bytes_off
            a2.limit = ap_bytes_off + nbytes
            return a2

        def run_norm(norm_args, AL, after_first=None, blocks=(0, 1, 2, 3)):
            n1, n2, toks = make_norm(*norm_args, AL=AL)
            bl = list(blocks)
            n1(bl[0])
            if after_first is not None:
                after_first()
            for i_ in range(1, len(bl)):
                n1(bl[i_]); n2(bl[i_ - 1])
            n2(bl[-1])
            return toks

        def conv_ffn(j, l, hn3, b_hn, norm_args=None, pre=None, tail_norm=None):
            m = A.mark()
            wup_off = A.top
            wup = [A.bf(KC * 256) for _ in range(3)]; b_wupP = [S.bufs(3), S.bufs(3)]
            if pre is not None:
                assert pre["off"] == wup_off, (pre["off"], wup_off)
                b_wupP = pre["tok"]
            wdn = [A.bf(20 * 128) for _ in range(2)]; b_wdn = S.bufs(2)
            act_off = A.top
            actT = A.bf(20 * 1024); actT3 = actT.rearrange("p (c t) -> p c t", t=1024)
            b_act = S.bufs(20)
            NACC = 4
            acc_off = A.top
            acc = [A.f32(1024) for _ in range(NACC)]; b_acc = S.bufs(NACC)
            sg = [A.f32(1024)]; b_sg = S.bufs(1)
            usb = [A.f32(1088) for _ in range(2)]; b_usb = S.bufs(2)
            acc_bytes = A.top - acc_off
            fcw3 = fcw[l].rearrange("p (c k) -> p c k", k=3)
            up_src = fup_d[l].rearrange("(kc p) n -> p kc n", p=128)
            dn_src = fdn_d[l].rearrange("(kc p) n -> p kc n", p=128)

            def load_up(g):
                if g >= 40:
                    return
                pj = g % 20
                ws = g % 3
                wv = wup[ws].rearrange("p (kc n) -> p kc n", n=256)
                ldc(wv[:, :, 0:128], up_src[:, :, pj * 128:(pj + 1) * 128], b_wupP[0][ws], f"wupA{ws}")
                ldc(wv[:, :, 128:256], up_src[:, :, FH + pj * 128:FH + (pj + 1) * 128], b_wupP[1][ws], f"wupG{ws}")

            def load_dn(gd):
                if gd >= 16:
                    return
                mc = gd % 8
                ds = gd % 2
                dv = wdn[ds].rearrange("p (kc n) -> p kc n", n=128)
                ldc(dv, dn_src[:, :, mc * 128:(mc + 1) * 128], b_wdn[ds], f"wdn{ds}")

            ntok = []
            if norm_args is not None:
                ntok = run_norm(norm_args, sub_arena(act_off, 20 * 1024 * 2), after_first=lambda: (load_up(0), load_up(1)))
            elif pre is None:
                load_up(0); load_up(1)
            ai = 0
            pending = []
            for hf in range(2):
                u0 = 0 if hf == 0 else 960
                for pj in range(20):
                    g = hf * 20 + pj
                    ws = g % 3
                    wv = wup[ws].rearrange("p (kc n) -> p kc n", n=256)
                    load_up(g + 2)
                    if pj == 16:
                        load_dn(hf * 8); load_dn(hf * 8 + 1)
                    accs = []
                    for part in range(2):
                        if part == 1 and pending:
                            pending.pop(0)()
                        pg = (pj * 2 + part) % 2
                        pbase = pg * 3
                        for kc in range(KC):
                            for nb, (c0, cn) in enumerate(((0, 512), (512, 512), (1024, 64))):
                                blkidx = sorted(set([(u0 + c0) // 512, (u0 + c0 + cn - 1) // 512]))
                                S.op("pe", lambda e, wv=wv, kc=kc, part=part, pbase=pbase, nb=nb, c0=c0, cn=cn, u0=u0: e.matmul(
                                    bank(pbase + nb, cn), lhsT=wv[:, kc, part * 128:(part + 1) * 128],
                                    rhs=hn3[:, kc, u0 + c0:u0 + c0 + cn], start=(kc == 0), stop=(kc == KC - 1)),
                                    reads=[b_wupP[part][ws]] + [t_ for b in blkidx for t_ in b_hn[b]], writes=[pb[pbase + nb]])
                        rb = [pb[pbase], pb[pbase + 1], pb[pbase + 2]]
                        ch = part * 20 + pj
                        a = ai % NACC
                        us = ai % 2
                        ai += 1
                        accs.append(a)
                        ups = usb[us]
                        ub = [b_usb[us]]
                        S.op("act", lambda e, ups=ups, pbase=pbase: e.copy(out=ups, in_=pst[:, pbase * 512: pbase * 512 + 1088]),
                             reads=rb, writes=ub)
                        ctr = 0 if hf == 0 else 64
                        S.op("act", lambda e, a=a, ups=ups, ctr=ctr, ch=ch: e.activation(
                            out=acc[a], in_=ups[:, ctr:ctr + 1024], func=AF.Identity,
                            scale=fcw3[:, ch, 1:2], bias=fcb[l][:, ch:ch + 1]),
                            reads=ub + [b_fc], writes=[b_acc[a]])
                        if hf == 0:
                            S.op("dve", lambda e, a=a, ups=ups, ch=ch: e.scalar_tensor_tensor(
                                out=acc[a][:, 64:1024], in0=ups[:, 0:960], scalar=fcw3[:, ch, 0:1], in1=acc[a][:, 64:1024],
                                op0=ALU.mult, op1=ALU.add), reads=ub + [b_fc, b_acc[a]], writes=[b_acc[a]])
                            S.op("dve", lambda e, a=a, ups=ups, ch=ch: e.scalar_tensor_tensor(
                                out=acc[a], in0=ups[:, 64:1088], scalar=fcw3[:, ch, 2:3], in1=acc[a],
                                op0=ALU.mult, op1=ALU.add), reads=ub + [b_fc, b_acc[a]], writes=[b_acc[a]])
                        else:
                            S.op("dve", lambda e, a=a, ups=ups, ch=ch: e.scalar_tensor_tensor(
                                out=acc[a], in0=ups[:, 0:1024], scalar=fcw3[:, ch, 0:1], in1=acc[a],
                                op0=ALU.mult, op1=ALU.add), reads=ub + [b_fc, b_acc[a]], writes=[b_acc[a]])
                            S.op("dve", lambda e, a=a, ups=ups, ch=ch: e.scalar_tensor_tensor(
                                out=acc[a][:, 0:960], in0=ups[:, 128:1088], scalar=fcw3[:, ch, 2:3], in1=acc[a][:, 0:960],
                                op0=ALU.mult, op1=ALU.add), reads=ub + [b_fc, b_acc[a]], writes=[b_acc[a]])
                    aa, ag = accs

                    def gate(aa=aa, ag=ag, pj=pj):
                        S.op("act", lambda e, ag=ag: e.activation(out=sg[0], in_=acc[ag], func=AF.Silu),
                             reads=[b_acc[ag]], writes=[b_sg[0]])
                        S.op("pool", lambda e, aa=aa, pj=pj: e.tensor_tensor(out=actT3[:, pj, :], in0=acc[aa], in1=sg[0], op=ALU.mult),
                             reads=[b_acc[aa], b_sg[0]], writes=[b_act[pj]] + ntok)
                    pending.append(gate)
                while pending:
                    pending.pop(0)()
                def dn_mm(mc, nb, bk, k0, k1):
                    ds = (hf * 8 + mc) % 2
                    dv = wdn[ds].rearrange("p (kc n) -> p kc n", n=128)
                    for kc in range(k0, k1):
                        S.op("pe", lambda e, dv=dv, kc=kc, nb=nb, bk=bk: e.matmul(
                            bank(bk), lhsT=dv[:, kc, :], rhs=actT3[:, kc, nb * 512:(nb + 1) * 512],
                            start=(kc == 0), stop=(kc == 19)),
                            reads=[b_wdn[ds], b_act[kc]], writes=[pb[bk]])

                def dn_evac(mc, nb, bk):
                    blk = hf * 2 + nb
                    sl = slice(blk * 512, (blk + 1) * 512)
                    S.op("dve", lambda e, bk=bk, mc=mc, sl=sl: e.scalar_tensor_tensor(
                        out=hT3[:, mc, sl], in0=bank(bk), scalar=modcol(l, 5, mc, j), in1=hT3[:, mc, sl],
                        op0=ALU.mult, op1=ALU.add), reads=[pb[bk], b_mod, b_hT[mc][blk]], writes=[b_hT[mc][blk]])

                head_groups = [(0, 0, 6), (0, 1, 7), (1, 0, 0)]
                for (mc, nb, bk) in head_groups:
                    dn_mm(mc, nb, bk, 0, 18)
                for (mc, nb, bk) in head_groups[0:2]:
                    dn_mm(mc, nb, bk, 18, 20)
                    dn_evac(mc, nb, bk)
                load_dn(hf * 8 + 2)
                dn_mm(1, 0, 0, 18, 20)
                dn_evac(1, 0, 0)
                dn_mm(1, 1, 1, 0, 20)
                dn_evac(1, 1, 1)
                load_dn(hf * 8 + 3)
                tn1 = tn2 = None
                if hf == 1 and tail_norm is not None:
                    tn1, tn2, _ = make_norm(*tail_norm, pbanks=(2, 3), AL=sub_arena(acc_off, acc_bytes),
                                            pre_w=b_acc + b_sg + b_usb)
                    tn1(0)
                for mc in range(2, KC):
                    gd = hf * 8 + mc
                    for nb in range(2):
                        bk = 6 + nb
                        dn_mm(mc, nb, bk, 0, 20)
                        dn_evac(mc, nb, bk)
                    if mc + 2 < KC:
                        load_dn(gd + 2)
                    if tn1 is not None:
                        if mc == 2:
                            tn1(1)
                        elif mc == 3:
                            tn2(0)
                        elif mc == 5:
                            tn2(1)
            A.release(m)

        class _Stop(Exception):
            pass

        def stop_at(tag):
            MARKS.append((tag, sum(1 for o in S.ops["pe"] if o["fn"] is not None)))
            if STOP == tag:
                raise _Stop()

        try:
          for j in range(2):
            S.barrier()
            A.release(base_mark)
            A.limit = A.n
            ogT = A.bf(KC * T); ogT3 = ogT.rearrange("p (kc t) -> p kc t", t=T)
            b_ogT = S.bufs(NT, "ogT")
            m_og = A.mark()
            hnE = A.bf(KC * E); hnE3 = hnE.rearrange("p (kc t) -> p kc t", t=E)
            b_hnE = S.bufs(NE, "hnE")
            wap = A.bf(KC * 64); b_wap = S.buf()
            wap3 = wap.rearrange("p (kc n) -> p kc n", n=64)
            ldc(wap3, wapad_d.rearrange("(kc p) n -> p kc n", p=128), b_wap, "wap")
            wq = A.bf(KC * 128); wk = A.bf(KC * 128); wvv = A.bf(KC * 256); wg = A.bf(KC * 256)
            b_wq, b_wk, b_wv, b_wg = S.bufs(4)
            wsrc = gwin_d.rearrange("(kc p) n -> p kc n", p=128)

            def load_head_w(h):
                ldc(wg.rearrange("p (kc n) -> p kc n", n=256), wsrc[:, :, 2048 + h * 256:2048 + (h + 1) * 256], b_wg, "wg")
                ldc(wq.rearrange("p (kc n) -> p kc n", n=128), wsrc[:, :, h * 128:(h + 1) * 128], b_wq, "wq")
                ldc(wk.rearrange("p (kc n) -> p kc n", n=128), wsrc[:, :, 512 + h * 128:512 + (h + 1) * 128], b_wk, "wk")
                ldc(wvv.rearrange("p (kc n) -> p kc n", n=256), wsrc[:, :, 1024 + h * 256:1024 + (h + 1) * 256], b_wv, "wv")
            load_head_w(0)
            mh = A.mark()
            xt = [A.f32(D) for _ in range(3)]; b_xt = S.bufs(3)
            junk = A.bf(D); b_junk = S.buf()
            xn = [A.bf(D) for _ in range(2)]; b_xn = S.bufs(2)
            ss = [A.f32(1) for _ in range(3)]; b_ss = S.bufs(3)
            rsd = [A.f32(1) for _ in range(3)]; b_rsd = S.bufs(3)

            def a1_load(e_):
                s = e_ % 3
                src = ctx2[j, e_ * 128:(e_ + 1) * 128, :] if e_ < 2 else x2[j, (e_ - 2) * 128:(e_ - 1) * 128, :]
                ld(xt[s], src, b_xt[s], f"xt{s}")

            def a1_stats(e_):
                s = e_ % 3
                s2 = e_ % 2
                S.op("act", lambda e, s=s: e.activation(out=junk, in_=xt[s], func=AF.Square, accum_out=ss[s]),
                     reads=[b_xt[s]], writes=[b_junk, b_ss[s]])
                rstd_from_ss(ss[s], rsd[s], [b_ss[s]], b_rsd[s], 1.0 / D)
                S.op("dve", lambda e, s=s, s2=s2: e.tensor_scalar(out=xn[s2], in0=xt[s], scalar1=rsd[s], scalar2=None, op0=ALU.mult),
                     reads=[b_xt[s], b_rsd[s]], writes=[b_xn[s2]])

            def a1_tr(e_):
                s2 = e_ % 2
                jj = 2 if e_ < 2 else j
                bk = 5 + (e_ % 3)
                ptb = bank(bk).bitcast(BF16).rearrange("p (kc t) -> p kc t", t=128)
                for kc in range(KC):
                    S.op("pe", lambda e, ptb=ptb, kc=kc, s2=s2: e.transpose(out=ptb[:, kc, :], in_=xn[s2][:, kc * 128:(kc + 1) * 128], identity=ident_b),
                         reads=[b_xn[s2], b_identb], writes=[pb[bk]])
                for kc in range(KC):
                    if kc < 2:
                        S.op("act", lambda e, ptb=ptb, kc=kc, e_=e_, jj=jj: e.activation(
                            out=hnE3[:, kc, e_ * 128:(e_ + 1) * 128], in_=ptb[:, kc, :], func=AF.Identity,
                            scale=acol(Amix, 0, kc, jj), bias=modcol(0, 0, kc, jj)),
                            reads=[pb[bk], b_Ader, b_mod], writes=[b_hnE[e_]])
                    else:
                        S.op("dve", lambda e, ptb=ptb, kc=kc, e_=e_, jj=jj: e.tensor_scalar(
                            out=hnE3[:, kc, e_ * 128:(e_ + 1) * 128], in0=ptb[:, kc, :],
                            scalar1=acol(Amix, 0, kc, jj), scalar2=modcol(0, 0, kc, jj), op0=ALU.mult, op1=ALU.add),
                            reads=[pb[bk], b_Ader, b_mod], writes=[b_hnE[e_]])

            a1_load(0); a1_load(1)
            if j == 0:
                adaH1 = [A.bf(KC * 512) for _ in range(2)]
                a_dma, a_mm, a_fin_part, _ = make_ada1(adaH1, S.bufs(2, "adaHa"))
                a_dma(0); a_dma(1)
            for s_ in range(NE + 1):
                if s_ < NE:
                    a1_stats(s_)
                if s_ >= 1:
                    a1_tr(s_ - 1)
                if s_ + 2 < NE:
                    a1_load(s_ + 2)
                if j == 0 and s_ in (6, 11, 16):
                    hp = {6: 0, 11: 2, 16: 4}[s_]
                    a_mm(hp); a_mm(hp + 1)
                    if hp + 2 < 6:
                        a_dma(hp + 2); a_dma(hp + 3)
            if j == 0:
                a_fin_part(0, 24)
            A.release(mh)
            dbg_dump(f"hnE{j}", hnE, [128, KC * E], BF16, b_hnE)
            stop_at("A1")

            stop_at("A1_")
            alow = A.bf(E); b_alow = S.buf()
            S.op("dve", lambda e: e.memset(alow[0:64, :], 1.0), writes=[b_alow])
            eblocks = [(0, 512), (512, 512), (1024, 512), (1536, 512), (2048, 256)]
            for bi, (c0, cn) in enumerate(eblocks):
                for kc in range(KC):
                    S.op("pe", lambda e, bi=bi, c0=c0, cn=cn, kc=kc: e.matmul(
                        pst[0:64, bi * 512: bi * 512 + cn], lhsT=wap3[:, kc, :], rhs=hnE3[:, kc, c0:c0 + cn],
                        start=(kc == 0), stop=(kc == KC - 1)),
                        reads=[b_wap] + b_hnE[c0 // 128:(c0 + cn) // 128], writes=[pb[bi]])
                for r0 in (0, 32):
                    S.op("dve", lambda e, bi=bi, c0=c0, cn=cn, r0=r0: e.tensor_copy(
                        out=alow[r0:r0 + 16, c0:c0 + cn], in_=pst[r0:r0 + 16, bi * 512: bi * 512 + cn]),
                        reads=[pb[bi]], writes=[b_alow])

            dead_off = A.top
            qT = A.bf(T); b_qT = S.buf()
            kT = A.bf(E); b_kT = S.buf()
            tmpA = A.f32(E); b_tA = S.buf()
            tmpB = A.f32(E); b_tB = S.buf()
            assert A.top - dead_off >= KC * D * 2 and A.top <= HT_OFF
            wo = ar[:, dead_off // 4: dead_off // 4 + KC * D // 2].bitcast(BF16)
            wo3 = wo.rearrange("p (kc n) -> p kc n", n=D); b_wo = S.buf()
            wo_end = dead_off + KC * D * 2
            vh = A.bf(NE * 256); vh3 = vh.rearrange("p (e n) -> p e n", n=256); b_vh = S.bufs(NE)
            sgh = A.bf(NT * 256); sgh3 = sgh.rearrange("p (e n) -> p e n", n=256); b_sgh = S.bufs(NT)
            tots = A.f32(NE); b_tots = S.buf()
            dec = [A.f32(NE) for _ in range(2)]; b_dec = S.bufs(2)
            qs = [A.bf(T) for _ in range(2)]; b_qs = S.bufs(2)
            kdT = [A.bf(E) for _ in range(2)]; b_kdT = S.bufs(2)
            kd = [A.bf(NE * 128) for _ in range(2)]
            kd3 = [k_.rearrange("p (e n) -> p e n", n=128) for k_ in kd]
            b_kd = [S.bufs(NE) for _ in range(2)]
            Stil = [A.bf(NT * 256) for _ in range(2)]
            Stil3 = [s_.rearrange("p (e n) -> p e n", n=256) for s_ in Stil]
            b_Stil = [S.bufs(NT) for _ in range(2)]
            Sst = [A.f32(256) for _ in range(2)]; b_S = S.bufs(2)
            attm = [A.bf(256) for _ in range(2)]; b_attm = S.bufs(2)
            ogt = [A.bf(256) for _ in range(2)]; b_ogt = S.bufs(2)
            junk2 = A.f32(256); b_junk2 = S.buf()
            ss2 = [A.f32(1) for _ in range(2)]; b_ss2 = S.bufs(2)
            rs2 = [A.f32(1) for _ in range(2)]; b_rs2 = S.bufs(2)
            xblocks = [(CT + 512 * b, 512) for b in range(4)]
            tmpA2 = A.f32(E); b_tA2 = S.buf()
            pbuf = [tmpA, tmpA2]; b_pbuf = [b_tA, b_tA2]
            Sst2 = [A.f32(256) for _ in range(2)]
            attm3 = A.bf(256); b_attm3 = S.buf()
            attmR = attm + [attm3]; b_attmR = b_attm + [b_attm3]
            ss2c = A.f32(1); rs2c = A.f32(1)
            ss2R = ss2 + [ss2c]; rs2R = rs2 + [rs2c]
            b_ss2R = b_ss2 + [S.buf()]; b_rs2R = b_rs2 + [S.buf()]
            SS = [[Sst[0], Sst2[0]], [Sst[1], Sst2[1]]]
            b_SS = [[S.buf(), S.buf()], [S.buf(), S.buf()]]
            for h in range(4):
                wq3 = wq.rearrange("p (kc n) -> p kc n", n=128)
                wk3 = wk.rearrange("p (kc n) -> p kc n", n=128)
                wv3 = wvv.rearrange("p (kc n) -> p kc n", n=256)
                wg3 = wg.rearrange("p (kc n) -> p kc n", n=256)
                if h > 0:
                    load_head_w(h)
                tB3 = tmpB.rearrange("p (c t) -> p c t", t=128)

                def gpair(pr):
                    bk = 5 + (pr % 3)
                    for half in range(2):
                        e_ = pr * 2 + half + 2
                        for kc in range(KC):
                            S.op("pe", lambda e, bk=bk, half=half, e_=e_, kc=kc: e.matmul(
                                bank(bk, 256, half * 256), lhsT=hnE3[:, kc, e_ * 128:(e_ + 1) * 128], rhs=wg3[:, kc, :],
                                start=(kc == 0), stop=(kc == KC - 1)), reads=[b_wg, b_hnE[e_]], writes=[pb[bk]])
                    S.op("act", lambda e, bk=bk, pr=pr: e.activation(out=sgh[:, pr * 512:(pr + 1) * 512], in_=bank(bk), func=AF.Silu),
                         reads=[pb[bk]], writes=[b_sgh[2 * pr], b_sgh[2 * pr + 1]])

                def vpair(pr):
                    bk = pr % 5
                    for half in range(2):
                        e_ = pr * 2 + half
                        for kc in range(KC):
                            S.op("pe", lambda e, bk=bk, half=half, e_=e_, kc=kc: e.matmul(
                                bank(bk, 256, half * 256), lhsT=hnE3[:, kc, e_ * 128:(e_ + 1) * 128], rhs=wv3[:, kc, :],
                                start=(kc == 0), stop=(kc == KC - 1)), reads=[b_wv, b_hnE[e_]], writes=[pb[bk]])
                    if pr % 2 == 1:
                        S.op("act", lambda e, bk=bk, pr=pr: e.copy(out=vh[:, pr * 512:(pr + 1) * 512], in_=bank(bk)),
                             reads=[pb[bk]], writes=[b_vh[2 * pr], b_vh[2 * pr + 1]])
                    else:
                        S.op("dve", lambda e, bk=bk, pr=pr: e.tensor_copy(out=vh[:, pr * 512:(pr + 1) * 512], in_=bank(bk)),
                             reads=[pb[bk]], writes=[b_vh[2 * pr], b_vh[2 * pr + 1]])

                def qproj():
                    for bi, (c0, cn) in enumerate(xblocks):
                        for kc in range(KC):
                            S.op("pe", lambda e, bi=bi, c0=c0, cn=cn, kc=kc: e.matmul(
                                bank(bi), lhsT=wq3[:, kc, :], rhs=hnE3[:, kc, c0:c0 + cn], start=(kc == 0), stop=(kc == KC - 1)),
                                reads=[b_wq] + b_hnE[c0 // 128:(c0 + cn) // 128], writes=[pb[bi]])
                        S.op("act", lambda e, bi=bi: e.activation(out=qT[:, bi * 512:(bi + 1) * 512], in_=bank(bi), func=AF.Identity, scale=128.0 ** -0.5),
                             reads=[pb[bi]], writes=[b_qT])

                def kproj():
                    for bi, (c0, cn) in enumerate(eblocks):
                        for kc in range(KC):
                            S.op("pe", lambda e, bi=bi, c0=c0, cn=cn, kc=kc: e.matmul(
                                bank(bi, cn), lhsT=wk3[:, kc, :], rhs=hnE3[:, kc, c0:c0 + cn], start=(kc == 0), stop=(kc == KC - 1)),
                                reads=[b_wk] + b_hnE[c0 // 128:(c0 + cn) // 128], writes=[pb[bi]])
                        S.op("dve", lambda e, bi=bi, c0=c0, cn=cn: e.tensor_copy(out=kT[:, c0:c0 + cn], in_=bank(bi, cn)),
                             reads=[pb[bi]], writes=[b_kT])

                def zmm(d):
                    for bi, (c0, cn) in enumerate(eblocks):
                        S.op("pe", lambda e, bi=bi, c0=c0, cn=cn, d=d, h=h: e.matmul(
                            bank(bi, cn), lhsT=wa2[32 * d:32 * d + 32, h * 128:(h + 1) * 128], rhs=alow[32 * d:32 * d + 32, c0:c0 + cn],
                            start=True, stop=True), reads=[b_wa2, b_alow], writes=[pb[bi]])
                        S.op("act", lambda e, bi=bi, c0=c0, cn=cn, d=d: e.activation(out=pbuf[d][:, c0:c0 + cn], in_=bank(bi, cn), func=AF.Exp, scale=-1.0),
                             reads=[pb[bi]], writes=[b_pbuf[d]])

                def lnp(d):
                    S.op("act", lambda e, d=d: e.activation(out=pbuf[d], in_=pbuf[d], func=AF.Ln, bias=onecol, scale=1.0),
                         reads=[b_pbuf[d], b_onecol], writes=[b_pbuf[d]])

                def scan(d):
                    S.op("dve", lambda e, d=d: e.tensor_tensor_scan(out=tmpB, data0=scanmask, data1=pbuf[d], initial=0.0, op0=ALU.mult, op1=ALU.add),
                         reads=[b_scanmask, b_pbuf[d]], writes=[b_tB])
                    S.op("dve", lambda e, tB3=tB3: e.tensor_copy(out=tots.unsqueeze(2), in_=tB3[:, :, 127:128]),
                         reads=[b_tB], writes=[b_tots])

                def decop(d):
                    S.op("act", lambda e, d=d: e.activation(out=dec[d], in_=tots, func=AF.Exp, scale=-1.0 / 16.0),
                         reads=[b_tots], writes=[b_dec[d]])

                def Rop(d):
                    if d == 0:
                        S.op("dve", lambda e, tB3=tB3: e.tensor_tensor(out=tB3, in0=tots.unsqueeze(2).to_broadcast([128, NE, 128]), in1=tB3, op=ALU.subtract),
                             reads=[b_tots, b_tB], writes=[b_tB])
                    else:
                        S.op("dve", lambda e: e.tensor_tensor(out=tmpB, in0=tmpB, in1=tmpA2, op=ALU.subtract),
                             reads=[b_tB, b_tA2], writes=[b_tB])

                def Eplus():
                    S.op("act", lambda e: e.activation(out=tmpA, in_=tmpB, func=AF.Exp, scale=1.0 / 16.0), reads=[b_tB], writes=[b_tA])

                def Eminus():
                    S.op("act", lambda e: e.activation(out=tmpA, in_=tmpB, func=AF.Exp, scale=-1.0 / 16.0), reads=[b_tB], writes=[b_tA])

                def qsmul(d):
                    S.op("dve", lambda e, d=d: e.tensor_tensor(out=qs[d], in0=qT, in1=tmpA[:, CT:E], op=ALU.mult),
                         reads=[b_qT, b_tA], writes=[b_qs[d]])

                def kdTmul(d):
                    S.op("dve", lambda e, d=d: e.tensor_tensor(out=kdT[d], in0=kT, in1=tmpA, op=ALU.mult),
                         reads=[b_kT, b_tA], writes=[b_kdT[d]])

                def trgroup(d, g4):
                    bk = 5 + (g4 % 3)
                    tiles = list(range(g4 * 4, min(NE, g4 * 4 + 4)))
                    ptb = bank(bk).bitcast(BF16).rearrange("p (c t) -> p c t", t=128)
                    for ti, e_ in enumerate(tiles):
                        S.op("pe", lambda e, ptb=ptb, ti=ti, e_=e_, d=d: e.transpose(out=ptb[:, ti, :], in_=kdT[d][:, e_ * 128:(e_ + 1) * 128], identity=ident_b),
                             reads=[b_kdT[d], b_identb], writes=[pb[bk]])
                    nt_ = len(tiles)
                    if True:
                        S.op("act", lambda e, ptb=ptb, nt_=nt_, g4=g4, d=d: e.copy(out=kd3[d][:, g4 * 4:g4 * 4 + nt_, :], in_=ptb[:, 0:nt_, :]),
                             reads=[pb[bk]], writes=[b_kd[d][e_] for e_ in tiles])
                    else:
                        S.op("dve", lambda e, ptb=ptb, nt_=nt_, g4=g4, d=d: e.tensor_copy(out=kd3[d][:, g4 * 4:g4 * 4 + nt_, :], in_=ptb[:, 0:nt_, :]),
                             reads=[pb[bk]], writes=[b_kd[d][e_] for e_ in tiles])

                for pr in range(NT // 2):
                    gpair(pr)
                zmm(0); zmm(1); lnp(0); lnp(1)
                qproj()
                scan(0); decop(0)
                kproj()
                Rop(0); Eplus(); qsmul(0); Eminus(); kdTmul(0)
                vpair(0); vpair(1)
                scan(1); vpair(2); decop(1); Rop(1); vpair(3); Eplus(); trgroup(0, 0); vpair(4); qsmul(1); Eminus()
                trgroup(0, 1); vpair(5); kdTmul(1); trgroup(0, 2); vpair(6); trgroup(0, 3); vpair(7); trgroup(0, 4); vpair(8)
                for g4 in range(5):
                    trgroup(1, g4)
                if h == 3:
                    S.op("pool", lambda e: e.dma_start(out=wo3, in_=gwout_d.rearrange("(kc p) n -> p kc n", p=128)),
                         writes=[b_qT, b_kT, b_tA, b_tB, b_wo], dma="wo")
                orders = [list(range(NE)), [1, 0] + list(range(NE - 1, 1, -1))]
                for d in range(2):
                    S.op("dve", lambda e, d=d: e.memset(SS[d][0], 0.0), writes=[b_SS[d][0]])
                cnt = 0
                for i in range(NE):
                    for d in range(2):
                        e_ = orders[d][i]
                        bk = cnt % 8; cnt += 1
                        cur, nxt = i % 2, (i + 1) % 2
                        S.op("pe", lambda e, bk=bk, e_=e_, d=d: e.matmul(bank(bk, 256), lhsT=kd3[d][:, e_, :], rhs=vh3[:, e_, :], start=True, stop=True),
                             reads=[b_kd[d][e_], b_vh[e_]], writes=[pb[bk]])
                        if e_ >= 2:
                            S.op("act", lambda e, e_=e_, d=d, cur=cur: e.activation(out=Stil3[d][:, e_ - 2, :], in_=SS[d][cur], func=AF.Identity, scale=dec[d][:, e_:e_ + 1]),
                                 reads=[b_SS[d][cur], b_dec[d]], writes=[b_Stil[d][e_ - 2]])
                        S.op("dve", lambda e, bk=bk, e_=e_, d=d, cur=cur, nxt=nxt: e.scalar_tensor_tensor(
                            out=SS[d][nxt], in0=SS[d][cur], scalar=dec[d][:, e_:e_ + 1], in1=bank(bk, 256), op0=ALU.mult, op1=ALU.add),
                            reads=[b_SS[d][cur], b_dec[d], pb[bk]], writes=[b_SS[d][nxt]])

                def o_s1(t_):
                    e_ = t_ + 2
                    bka = 5 + (t_ % 3)
                    s3 = t_ % 3
                    for d in range(2):
                        S.op("pe", lambda e, bka=bka, d=d, e_=e_, t_=t_: e.matmul(
                            bank(bka, 128, d * 128), lhsT=kdT[d][:, e_ * 128:(e_ + 1) * 128], rhs=qs[d][:, t_ * 128:(t_ + 1) * 128], start=True, stop=True),
                            reads=[b_kdT[d], b_qs[d]], writes=[pb[bka]])
                    S.op("dve", lambda e, bka=bka, s3=s3: e.tensor_tensor(out=attmR[s3], in0=bank(bka, 256), in1=maskUL, op=ALU.mult),
                         reads=[pb[bka], b_mask], writes=[b_attmR[s3]])

                def o_s2(t_):
                    e_ = t_ + 2
                    bko = t_ % 4
                    s3 = t_ % 3
                    for d in range(2):
                        S.op("pe", lambda e, bko=bko, d=d, t_=t_: e.matmul(
                            bank(bko, 256), lhsT=qs[d][:, t_ * 128:(t_ + 1) * 128], rhs=Stil3[d][:, t_, :], start=(d == 0), stop=False),
                            reads=[b_qs[d], b_Stil[d][t_]], writes=[pb[bko]])
                    for d in range(2):
                        S.op("pe", lambda e, bko=bko, d=d, s3=s3, e_=e_: e.matmul(
                            bank(bko, 256), lhsT=attmR[s3][:, d * 128:(d + 1) * 128], rhs=vh3[:, e_, :], start=False, stop=(d == 1)),
                            reads=[b_attmR[s3], b_vh[e_]], writes=[pb[bko]])
                    S.op("act", lambda e, bko=bko, s3=s3: e.activation(out=junk2, in_=bank(bko, 256), func=AF.Square, accum_out=ss2R[s3]),
                         reads=[pb[bko]], writes=[b_junk2, b_ss2R[s3]])
                    rstd_from_ss(ss2R[s3], rs2R[s3], [b_ss2R[s3]], b_rs2R[s3], 1.0 / 256.0)

                def o_s3(t_):
                    bko = t_ % 4
                    s3 = t_ % 3
                    s = t_ % 2
                    S.op("dve", lambda e, bko=bko, s=s, s3=s3, t_=t_: e.scalar_tensor_tensor(
                        out=ogt[s], in0=bank(bko, 256), scalar=rs2R[s3], in1=sgh3[:, t_, :], op0=ALU.mult, op1=ALU.mult),
                        reads=[pb[bko], b_rs2R[s3], b_sgh[t_]], writes=[b_ogt[s]])

                def o_s4(t_):
                    bka = 5 + (t_ % 3)
                    s = t_ % 2
                    ptb = bank(bka).bitcast(BF16)[:, 512:768].rearrange("p (c t) -> p c t", t=128)
                    for cc in range(2):
                        S.op("pe", lambda e, ptb=ptb, cc=cc, s=s: e.transpose(out=ptb[:, cc, :], in_=ogt[s][:, cc * 128:(cc + 1) * 128], identity=ident_b),
                             reads=[b_ogt[s], b_identb], writes=[pb[bka]])
                    S.op("act", lambda e, ptb=ptb, h=h, t_=t_: e.copy(out=ogT3[:, 2 * h:2 * h + 2, t_ * 128:(t_ + 1) * 128], in_=ptb),
                         reads=[pb[bka]], writes=[b_ogT[t_]])

                for s_ in range(NT + 3):
                    if h == 3 and 4 <= s_ < 12:
                        kc_ = s_ - 4
                        S.op("dve", lambda e, kc_=kc_: e.tensor_scalar(out=wo3[:, kc_, :], in0=wo3[:, kc_, :], scalar1=hgT[:, kc_:kc_ + 1], scalar2=None, op0=ALU.mult),
                             reads=[b_wo, b_hgT], writes=[b_wo])
                    if s_ < NT:
                        o_s1(s_)
                    if 0 <= s_ - 1 < NT:
                        o_s2(s_ - 1)
                    if 0 <= s_ - 2 < NT:
                        o_s3(s_ - 2)
                    if 0 <= s_ - 3 < NT:
                        o_s4(s_ - 3)
            dbg_dump(f"ogT{j}", ogT, [128, KC * T], BF16, b_ogT)
            stop_at("A3")

            S.barrier()
            A.release(m_og)
            A.limit = HT_OFF
            NXT = 3
            xt_off = A.top
            xt = [A.f32(D) for _ in range(NXT)]; b_xt = S.bufs(NXT)
            hn3 = ogT3
            b_hn = [[S.buf(), S.buf()] for _ in range(4)]
            fn1, fn2, _ = make_norm(j, Affn, 3, 0, hn3, b_hn, extra_w=[b_ogT[b * 4:b * 4 + 4] for b in range(4)], pbanks=(0, 1))
            ada_dma = ada_mm = ada_fin = None
            if j == 0:
                assert wo_end + 2 * KC * 512 * 2 <= HT_OFF
                adaH = [ar[:, (wo_end + i_ * KC * 512 * 2) // 4: (wo_end + i_ * KC * 512 * 2) // 4 + KC * 512 // 2].bitcast(BF16)
                        for i_ in range(2)]
                ada_dma, ada_mm, ada_fin_part, ada_fin = make_ada1(adaH, S.bufs(2, "adaH"))
                ada_dma(6); ada_dma(7)
            assert A.top <= dead_off
            ada_hp = [6]
            ada_calls = [0]

            def ada_step():
                ada_calls[0] += 1
                if ada_mm is None or ada_hp[0] >= 12 or ada_calls[0] % 2 == 0:
                    return
                hp = ada_hp[0]
                ada_mm(hp); ada_mm(hp + 1)
                ada_dma(hp + 2); ada_dma(hp + 3)
                ada_hp[0] += 2
            for t_ in range(NT):
                s = t_ % NXT
                ld(xt[s], x2[j, t_ * 128:(t_ + 1) * 128, :], b_xt[s], f"xt{s}")
                for hb in range(2):
                    bk = (t_ * 2 + hb) % 4
                    for c4 in range(4):
                        kc = hb * 4 + c4
                        S.op("pe", lambda e, bk=bk, c4=c4, kc=kc, s=s: e.transpose(out=bank(bk, 128, c4 * 128), in_=xt[s][:, kc * 128:(kc + 1) * 128], identity=ident_f),
                             reads=[b_xt[s], b_identf], writes=[pb[bk]])
                    src = bank(bk).rearrange("p (c t) -> p c t", t=128)
                    dst = hT3[:, hb * 4:hb * 4 + 4, t_ * 128:(t_ + 1) * 128]
                    wb = [b_hT[hb * 4 + c4][t_ // 4] for c4 in range(4)]
                    if hb == 0:
                        S.op("act", lambda e, src=src, dst=dst: e.copy(out=dst, in_=src), reads=[pb[bk]], writes=wb)
                    else:
                        S.op("dve", lambda e, src=src, dst=dst: e.tensor_copy(out=dst, in_=src), reads=[pb[bk]], writes=wb)

            b_wupB = [S.bufs(3), S.bufs(3)]
            up_src0 = fup_d[0].rearrange("(kc p) n -> p kc n", p=128)
            for g_ in range(2):
                o_ = xt_off + g_ * KC * 256 * 2
                wvp = ar[:, o_ // 4: o_ // 4 + KC * 256 // 2].bitcast(BF16).rearrange("p (kc n) -> p kc n", n=256)
                S.op("pool", lambda e, wvp=wvp, g_=g_: e.dma_start(out=wvp[:, :, 0:128], in_=up_src0[:, :, g_ * 128:(g_ + 1) * 128]),
                     writes=[b_wupB[0][g_]] + b_xt, dma=f"wupA{g_}")
                S.op("pool", lambda e, wvp=wvp, g_=g_: e.dma_start(out=wvp[:, :, 128:256], in_=up_src0[:, :, FH + g_ * 128:FH + (g_ + 1) * 128]),
                     writes=[b_wupB[1][g_]] + b_xt, dma=f"wupG{g_}")

            def outproj(blk, mcs=range(KC)):
                sl = slice(blk * 512, (blk + 1) * 512)
                for mc in mcs:
                    bk = 4 + (mc % 4)
                    for kc in range(KC):
                        S.op("pe", lambda e, bk=bk, mc=mc, kc=kc, sl=sl: e.matmul(
                            bank(bk), lhsT=wo3[:, kc, mc * 128:(mc + 1) * 128], rhs=ogT3[:, kc, sl], start=(kc == 0), stop=(kc == KC - 1)),
                            reads=[b_wo] + b_ogT[blk * 4:blk * 4 + 4], writes=[pb[bk]])
                    S.op("dve", lambda e, bk=bk, mc=mc, sl=sl: e.scalar_tensor_tensor(
                        out=hT3[:, mc, sl], in0=bank(bk), scalar=modcol(0, 2, mc, j), in1=hT3[:, mc, sl], op0=ALU.mult, op1=ALU.add),
                        reads=[pb[bk], b_mod, b_hT[mc][blk]], writes=[b_hT[mc][blk]])

            ada_step()
            for blk in range(4):
                outproj(blk, range(0, 4))
                if blk >= 1:
                    fn1(blk - 1); fn2(blk - 1)
                outproj(blk, range(4, KC))
                ada_step()
            fn1(3); fn2(3)
            ada_step()
            if ada_fin is not None:
                assert ada_hp[0] == 12, ada_hp[0]
                ada_fin_part(24, 48)
                ada_fin()
            dbg_dump(f"hA{j}", hT, [128, KC * T], F32, [b for r in b_hT for b in r])
            stop_at("A4")

            S.barrier()
            A.release(base_mark)
            hn = A.bf(KC * T)
            assert hn.offset == ogT.offset
            conv_ffn(j, 0, hn3, b_hn, pre={"tok": b_wupB, "off": xt_off}, tail_norm=(j, Amix, 0, 1, hn3, b_hn))
            dbg_dump(f"hB{j}", hT, [128, KC * T], F32, [b for r in b_hT for b in r])
            stop_at("B")

            S.barrier()
            mC = A.mark()
            wsc_off = A.top
            wsc = [A.bf(KC * 384) for _ in range(2)]; b_wscP = [S.bufs(2), S.bufs(2), S.bufs(2)]
            wo = A.bf(KC * D); wo3 = wo.rearrange("p (kc n) -> p kc n", n=D); b_wo = S.buf()
            yp_off = A.top
            yp = A.bf(KC * T); yp3 = yp.rearrange("p (kc t) -> p kc t", t=T); b_yp = S.bufs(KC)
            scsrc = scwin_d.rearrange("(kc p) n -> p kc n", p=128)

            def load_sc(c):
                if c >= KC:
                    return
                s_ = c % 2
                w3_ = wsc[s_].rearrange("p (kc n) -> p kc n", n=384)
                for part in range(3):
                    ldc(w3_[:, :, part * 128:(part + 1) * 128], scsrc[:, :, part * D + c * 128: part * D + (c + 1) * 128], b_wscP[part][s_], f"wsc{part}_{s_}")

            def c_loads():
                load_sc(0); load_sc(1)
                ldc(wo3, scwout_d.rearrange("(kc p) n -> p kc n", p=128), b_wo, "wo2")
            ntokC = run_norm((j, Amix, 0, 1, hn3, b_hn), sub_arena(yp_off, KC * T * 2), after_first=c_loads, blocks=(2, 3))
            stop_at("Cnorm")
            c_off = A.top
            Csb = A.f32(T); b_C = S.buf()
            cv = A.f32(T); b_cv = S.buf()
            co = A.f32(T); b_co = S.buf()
            scw3 = scw.rearrange("p (c k) -> p c k", k=3)
            cv3 = cv.rearrange("p (r w) -> p r w", w=64)
            co3 = co.rearrange("p (r w) -> p r w", w=64)
            pgi = 0
            for c in range(KC):
                s = c % 2
                w3 = wsc[s].rearrange("p (kc n) -> p kc n", n=384)
                for part in (1, 2, 0):
                    pg = pgi % 2; pgi += 1
                    for blk in range(4):
                        for kc in range(KC):
                            S.op("pe", lambda e, w3=w3, part=part, pg=pg, blk=blk, kc=kc: e.matmul(
                                bank(pg * 4 + blk), lhsT=w3[:, kc, part * 128:(part + 1) * 128], rhs=hn3[:, kc, blk * 512:(blk + 1) * 512],
                                start=(kc == 0), stop=(kc == KC - 1)), reads=[b_wscP[part][s]] + b_hn[blk], writes=[pb[pg * 4 + blk]])
                    pall = pst[:, pg * 2048:(pg + 1) * 2048]
                    rb = pb[pg * 4:pg * 4 + 4]
                    if part == 1:
                        S.op("act", lambda e, pall=pall: e.copy(out=Csb, in_=pall), reads=rb, writes=[b_C])
                    elif part == 2:
                        S.op("dve", lambda e, pall=pall: e.tensor_tensor(out=cv, in0=pall, in1=Csb, op=ALU.mult), reads=rb + [b_C], writes=[b_cv])
                        S.op("act", lambda e, c=c: e.activation(out=co, in_=cv, func=AF.Identity, scale=scw3[:, c, 1:2]), reads=[b_cv, b_scw], writes=[b_co])
                        S.op("dve", lambda e, c=c: e.scalar_tensor_tensor(out=co3[:, :, 1:64], in0=cv3[:, :, 0:63], scalar=scw3[:, c, 0:1], in1=co3[:, :, 1:64],
                                                                          op0=ALU.mult, op1=ALU.add), reads=[b_cv, b_scw, b_co], writes=[b_co])
                        S.op("dve", lambda e, c=c: e.scalar_tensor_tensor(out=co3[:, :, 0:63], in0=cv3[:, :, 1:64], scalar=scw3[:, c, 2:3], in1=co3[:, :, 0:63],
                                                                          op0=ALU.mult, op1=ALU.add), reads=[b_cv, b_scw, b_co], writes=[b_co])
                    else:
                        S.op("dve", lambda e, pall=pall, c=c: e.tensor_tensor(out=yp3[:, c, :], in0=pall, in1=co, op=ALU.mult), reads=rb + [b_co], writes=[b_yp[c]] + ntokC)
                        load_sc(c + 2)
            dead_tok = [t_ for p_ in b_wscP for t_ in p_] + [b_C, b_cv, b_co]
            dn1, dn2, _ = make_norm(j, Affn, 3, 1, hn3, b_hn, pbanks=(0, 1),
                                    AL=ChainArena([sub_arena(wsc_off, 2 * KC * 384 * 2), sub_arena(c_off, 3 * T * 4)]), pre_w=dead_tok)
            oc = [0]

            def c_outproj(blk, mcs=range(KC)):
                sl = slice(blk * 512, (blk + 1) * 512)
                for mc in mcs:
                    bk = 2 + (oc[0] % 6); oc[0] += 1
                    for kc in range(KC):
                        S.op("pe", lambda e, bk=bk, mc=mc, kc=kc, sl=sl: e.matmul(
                            bank(bk), lhsT=wo3[:, kc, mc * 128:(mc + 1) * 128], rhs=yp3[:, kc, sl], start=(kc == 0), stop=(kc == KC - 1)),
                            reads=[b_wo, b_yp[kc]], writes=[pb[bk]])
                    S.op("dve", lambda e, bk=bk, mc=mc, sl=sl: e.scalar_tensor_tensor(
                        out=hT3[:, mc, sl], in0=bank(bk), scalar=modcol(1, 2, mc, j), in1=hT3[:, mc, sl], op0=ALU.mult, op1=ALU.add),
                        reads=[pb[bk], b_mod, b_hT[mc][blk]], writes=[b_hT[mc][blk]])

            for blk in range(4):
                c_outproj(blk, range(0, 4))
                if blk >= 1:
                    dn1(blk - 1); dn2(blk - 1)
                c_outproj(blk, range(4, KC))
            dn1(3); dn2(3)
            A.release(mC)
            dbg_dump(f"hC{j}", hT, [128, KC * T], F32, [b for r in b_hT for b in r])
            stop_at("C")

            S.barrier()
            conv_ffn(j, 1, hn3, b_hn)
            stop_at("D")

            S.barrier()
            A.release(base_mark)
            NOT = 4
            ot = [A.f32(D) for _ in range(NOT)]; b_ot = S.bufs(NOT)
            junk3 = A.bf(D); b_junk3 = S.buf()
            ss3 = [A.f32(1) for _ in range(3)]; b_ss3 = S.bufs(3)
            rs3 = [A.f32(1) for _ in range(3)]; b_rs3 = S.bufs(3)

            def e_s1(t_):
                pg = t_ % 4
                for kc in range(KC):
                    bk = pg * 2 + kc // 4
                    S.op("pe", lambda e, bk=bk, kc=kc, t_=t_: e.transpose(out=bank(bk, 128, (kc % 4) * 128), in_=hT3[:, kc, t_ * 128:(t_ + 1) * 128], identity=ident_f),
                         reads=[b_hT[kc][t_ // 4], b_identf], writes=[pb[bk]])

            def e_s2(t_):
                pg = t_ % 4
                s3 = t_ % 3
                pall = pst[:, pg * 1024:(pg + 1) * 1024]
                S.op("act", lambda e, pall=pall, s3=s3: e.activation(out=junk3, in_=pall, func=AF.Square, accum_out=ss3[s3]),
                     reads=[pb[pg * 2], pb[pg * 2 + 1]], writes=[b_junk3, b_ss3[s3]])
                rstd_from_ss(ss3[s3], rs3[s3], [b_ss3[s3]], b_rs3[s3], 1.0 / D)

            def e_s3(t_):
                pg = t_ % 4
                s3 = t_ % 3
                s = t_ % NOT
                pall = pst[:, pg * 1024:(pg + 1) * 1024]
                S.op("dve", lambda e, pall=pall, s=s, s3=s3: e.scalar_tensor_tensor(out=ot[s], in0=pall, scalar=rs3[s3], in1=fnorm, op0=ALU.mult, op1=ALU.mult),
                     reads=[pb[pg * 2], pb[pg * 2 + 1], b_rs3[s3], b_fnorm], writes=[b_ot[s]])
                S.op("sp", lambda e, s=s, t_=t_: e.dma_start(out=out2[j, t_ * 128:(t_ + 1) * 128, :], in_=ot[s]), reads=[b_ot[s]], dma=f"ot{s}")

            for s_ in range(NT + 2):
                if s_ < NT:
                    e_s1(s_)
                if 0 <= s_ - 1 < NT:
                    e_s2(s_ - 1)
                if 0 <= s_ - 2 < NT:
                    e_s3(s_ - 2)
            stop_at("E")
        except _Stop:
            pass
        S.barrier()
        counts = S.emit(st)
        print("op counts", counts, flush=True)
    return nc, dbg


def prep_inputs(inp):
    f = lambda a: np.ascontiguousarray(np.asarray(a, dtype=np.float32))
    x = f(inp["x"]); c = f(inp["c"]); ctx = f(inp["ctx"]); c_ctx = f(inp["c_ctx"])
    pm = lambda v, n: np.ascontiguousarray(v.reshape(n, 128).T)
    shared = {}
    shared["ada_w"] = f(inp["ada_w"])
    ada_b = f(inp["ada_b"])
    shared["adab"] = np.stack([pm(ada_b[l], 48) for l in range(2)])
    shared["nmix"] = np.stack([pm(f(inp["norm_mix"])[l], 8) for l in range(2)])
    shared["nffn"] = np.stack([pm(f(inp["norm_ffn"])[l], 8) for l in range(2)])
    shared["fnorm"] = np.ascontiguousarray(np.broadcast_to(f(inp["final_norm"])[None, :], (128, D)))
    gw = f(inp["gla_w_in"])[0]
    shared["gla_w_in"] = gw
    wap = np.zeros((D, 64), np.float32)
    wap[:, 0:16] = gw[:, 3072:3088]
    wap[:, 32:48] = gw[:, 3088:3104]
    shared["wa_pad"] = wap
    wa2 = np.zeros((64, 512), np.float32)
    w_a2 = f(inp["gla_w_a2"])[0]; b_a = f(inp["gla_b_a"])[0]
    wa2[0:16] = w_a2[0]; wa2[16] = b_a[0]
    wa2[32:48] = w_a2[1]; wa2[48] = b_a[1]
    shared["wa2aug"] = wa2
    hg = f(inp["gla_head_norm"])[0]
    shared["hgT"] = np.ascontiguousarray(np.tile(hg.reshape(2, 128).T, (1, 4)))
    shared["gla_w_out"] = f(inp["gla_w_out"])[0]
    shared["sc_w_in"] = f(inp["sc_w_in"])[0]
    scw = f(inp["sc_conv_w"])[0]
    shared["scw"] = np.ascontiguousarray(scw.reshape(3, 8, 128).transpose(2, 1, 0).reshape(128, 24))
    shared["sc_w_out"] = f(inp["sc_w_out"])[0]
    shared["ffn_w_up"] = f(inp["ffn_w_up"])
    fcw = f(inp["ffn_conv_w"])
    shared["fcw"] = np.ascontiguousarray(fcw.reshape(2, 3, 40, 128).transpose(0, 3, 2, 1).reshape(2, 128, 120))
    fcb = f(inp["ffn_conv_b"])
    shared["fcb"] = np.ascontiguousarray(fcb.reshape(2, 40, 128).transpose(0, 2, 1))
    shared["ffn_w_down"] = f(inp["ffn_w_down"])
    shared["ident"] = np.eye(128, dtype=np.float32)
    shared["ones"] = np.ones((128, 128), np.float32)
    jj, ii = np.meshgrid(np.arange(128), np.arange(128), indexing="ij")
    shared["maskUL"] = np.concatenate([(jj <= ii), (jj >= ii)], axis=1).astype(np.float32)
    in_maps = []
    for core in range(NCORES):
        m = dict(shared)
        m["x2"] = x[2 * core:2 * core + 2]
        m["ctx2"] = ctx[2 * core:2 * core + 2]
        cc = np.stack([c[2 * core], c[2 * core + 1], c_ctx], axis=0)
        m["cT"] = np.ascontiguousarray(cc.reshape(3, 8, 128).transpose(2, 1, 0).reshape(128, 24))
        in_maps.append(m)
    return in_maps


_NC_CACHE = {}


def kernel(**inputs):
    in_maps = prep_inputs(inputs)
    if "nc" not in _NC_CACHE:
        _NC_CACHE["nc"] = build_nc(False)[0]
    res = run_bass_kernel_spmd(_NC_CACHE["nc"], in_maps, core_ids=list(range(NCORES)))
    out = np.concatenate([np.asarray(r["out2"]) for r in res.results], axis=0)
    return out.astype(np.float32)
```
